# Optimizing a Trainium2 kernel written in Bass

```python
import math
import jax, jax.numpy as jnp
from jax import lax
import numpy as np

D_MODEL = 1024
BATCH = 32
SEQ = 2048
DEPTH = 1

SSD_D_INNER = D_MODEL
SSD_HEAD_DIM = 64
SSD_N_HEADS = SSD_D_INNER // SSD_HEAD_DIM
SSD_N_GROUPS = 2
SSD_D_STATE = 128
SSD_CONV_WIDTH = 4
SSD_CHUNK = 128
SSD_CONV_DIM = SSD_D_INNER + 2 * SSD_N_GROUPS * SSD_D_STATE
ATTN_HEAD_DIM = 64
ATTN_N_HEADS = D_MODEL // (2 * ATTN_HEAD_DIM)
ATTN_WIDTH = ATTN_N_HEADS * 2 * ATTN_HEAD_DIM
ROPE_THETA = 10000.0
Q_BLOCK = 128
D_FF = 2816
FFN_CONV_WIDTH = 3
NORM_EPS = 1e-6
SUBLN_EPS = 1e-5
IN_SIZES = (SSD_D_INNER, SSD_CONV_DIM, SSD_N_HEADS, ATTN_WIDTH, ATTN_WIDTH, ATTN_WIDTH, D_MODEL, D_MODEL)
IN_COLS = SSD_D_INNER + SSD_CONV_DIM + SSD_N_HEADS + 3 * ATTN_WIDTH + 2 * D_MODEL

kernel_name = "hybrid_ssd_diffattn_convffn"


def _split_points(sizes):
    pts, acc = [], 0
    for s in sizes[:-1]:
        acc += s
        pts.append(acc)
    return pts


def rms_norm(x, w, eps=NORM_EPS):
    xf = x.astype(jnp.float32)
    y = xf * lax.rsqrt(jnp.mean(xf * xf, axis=-1, keepdims=True) + eps)
    return (y * w.astype(jnp.float32)).astype(x.dtype)


def causal_dwconv(x, w, b):
    k, c = w.shape
    y = lax.conv_general_dilated(
        x, w[:, None, :].astype(x.dtype), window_strides=(1,), padding=[(k - 1, 0)],
        dimension_numbers=("NWC", "WIO", "NWC"), feature_group_count=c)
    return y + b.astype(x.dtype)


def rope_tables(seq):
    inv = 1.0 / (ROPE_THETA ** (jnp.arange(0, ATTN_HEAD_DIM, 2, dtype=jnp.float32) / ATTN_HEAD_DIM))
    ang = jnp.arange(seq, dtype=jnp.float32)[:, None] * inv[None, :]
    return jnp.cos(ang), jnp.sin(ang)


def apply_rope(x, cos, sin):
    half = ATTN_HEAD_DIM // 2
    xf = x.astype(jnp.float32)
    x1, x2 = xf[..., :half], xf[..., half:]
    c = cos[None, :, None, None, :]
    s = sin[None, :, None, None, :]
    return jnp.concatenate([x1 * c - x2 * s, x2 * c + x1 * s], axis=-1).astype(x.dtype)


def ssd_chunked(xdt, dA, Bm, Cm):
    b, s, h, p = xdt.shape
    g, n = SSD_N_GROUPS, SSD_D_STATE
    e = h // g
    L = SSD_CHUNK
    c = s // L
    X = xdt.reshape(b, c, L, g, e, p)
    Bc = Bm.reshape(b, c, L, g, n)
    Cc = Cm.reshape(b, c, L, g, n)
    A_cs = jnp.cumsum(dA.reshape(b, c, L, g, e), axis=2).transpose(0, 1, 3, 4, 2)
    causal = jnp.arange(L)[:, None] >= jnp.arange(L)[None, :]
    seg = A_cs[..., :, None] - A_cs[..., None, :]
    Lmat = jnp.exp(jnp.where(causal, seg, -jnp.inf))
    CB = jnp.einsum("bclgn,bcsgn->bcgls", Cc, Bc)
    y_diag = jnp.einsum("bcgls,bcgels,bcsgep->bclgep", CB, Lmat, X)
    decay_states = jnp.exp(A_cs[..., -1:] - A_cs)
    states = jnp.einsum("bclgn,bcgel,bclgep->bcgepn", Bc, decay_states, X)
    chunk_tot = jnp.pad(A_cs[..., -1].transpose(0, 2, 3, 1), ((0, 0), (0, 0), (0, 0), (1, 0)))
    ccs = jnp.cumsum(chunk_tot, axis=-1)
    cmask = jnp.arange(c + 1)[:, None] >= jnp.arange(c + 1)[None, :]
    decay_chunk = jnp.exp(jnp.where(cmask, ccs[..., :, None] - ccs[..., None, :], -jnp.inf))
    states_pad = jnp.concatenate([jnp.zeros_like(states[:, :1]), states], axis=1)
    new_states = jnp.einsum("bgezw,bwgepn->bzgepn", decay_chunk, states_pad)
    prev = new_states[:, :-1]
    y_off = jnp.einsum("bclgn,bcgepn,bcgel->bclgep", Cc, prev, jnp.exp(A_cs))
    return (y_diag + y_off).reshape(b, s, h, p)


def ssd_branch(z, xbc, dt_raw, conv_w, conv_b, dt_bias, a_log, d_skip, norm_w):
    b, s, _ = z.shape
    xbc = jax.nn.silu(causal_dwconv(xbc, conv_w, conv_b))
    gn = SSD_N_GROUPS * SSD_D_STATE
    xs, Bm, Cm = jnp.split(xbc, [SSD_D_INNER, SSD_D_INNER + gn], axis=-1)
    xs = xs.reshape(b, s, SSD_N_HEADS, SSD_HEAD_DIM).astype(jnp.float32)
    Bm = Bm.reshape(b, s, SSD_N_GROUPS, SSD_D_STATE).astype(jnp.float32)
    Cm = Cm.reshape(b, s, SSD_N_GROUPS, SSD_D_STATE).astype(jnp.float32)
    dt = jax.nn.softplus(dt_raw.astype(jnp.float32) + dt_bias.astype(jnp.float32))
    A = -jnp.exp(a_log.astype(jnp.float32))
    y = ssd_chunked(xs * dt[..., None], dt * A, Bm, Cm)
    y = (y + xs * d_skip.astype(jnp.float32)[:, None]).reshape(b, s, SSD_D_INNER)
    yg = (y * jax.nn.silu(z.astype(jnp.float32))).reshape(b, s, SSD_N_GROUPS, SSD_D_INNER // SSD_N_GROUPS)
    yg = yg * lax.rsqrt(jnp.mean(yg * yg, axis=-1, keepdims=True) + NORM_EPS)
    yg = yg.reshape(b, s, SSD_D_INNER) * norm_w.astype(jnp.float32)
    return yg.astype(z.dtype)


def diff_attn_branch(q, k, v, lq1, lk1, lq2, lk2, subln_w, lambda_init):
    b, s, _ = q.shape
    H, d = ATTN_N_HEADS, ATTN_HEAD_DIM
    cos, sin = rope_tables(s)
    q = apply_rope(q.reshape(b, s, H, 2, d), cos, sin) * (d ** -0.5)
    k = apply_rope(k.reshape(b, s, H, 2, d), cos, sin)
    qt = q.transpose(0, 2, 3, 1, 4)
    kt = k.transpose(0, 2, 3, 1, 4)
    vt = v.reshape(b, s, H, 2 * d).transpose(0, 2, 1, 3)
    f32 = jnp.float32
    lam = (jnp.exp(jnp.sum(lq1.astype(f32) * lk1.astype(f32)))
           - jnp.exp(jnp.sum(lq2.astype(f32) * lk2.astype(f32))) + lambda_init)
    kpos = jnp.arange(s)

    def query_block(i):
        qb = lax.dynamic_slice_in_dim(qt, i * Q_BLOCK, Q_BLOCK, axis=3)
        sc = jnp.einsum("bhiqd,bhikd->bhiqk", qb, kt).astype(f32)
        qpos = i * Q_BLOCK + jnp.arange(Q_BLOCK)
        sc = jnp.where(kpos[None, :] <= qpos[:, None], sc, -jnp.inf)
        p = jax.nn.softmax(sc, axis=-1)
        attn = p[:, :, 0] - lam * p[:, :, 1]
        return jnp.einsum("bhqk,bhke->bhqe", attn.astype(vt.dtype), vt)

    o = lax.map(query_block, jnp.arange(s // Q_BLOCK))
    o = o.transpose(1, 0, 3, 2, 4).reshape(b, s, H, 2 * d)
    o = rms_norm(o, subln_w, SUBLN_EPS) * (1.0 - lambda_init)
    return o.reshape(b, s, ATTN_WIDTH)


def conv_ffn(h, w_up, conv_w, conv_b, w_down):
    gate, val = jnp.split(h @ w_up, [D_FF], axis=-1)
    gate = causal_dwconv(gate, conv_w, conv_b)
    return (jax.nn.silu(gate) * val) @ w_down


def setup_inputs(seed: int = 0) -> dict:
    key = jax.random.key(seed)
    ks = jax.random.split(key, 24)
    f32 = jnp.float32

    def nrm(k, shape, scale):
        return jax.random.normal(k, shape, f32) * scale

    def gain(k, shape):
        return 1.0 + 0.01 * jax.random.normal(k, shape, f32)

    dt = jnp.exp(jax.random.uniform(ks[5], (DEPTH, SSD_N_HEADS), f32)
                 * (math.log(0.1) - math.log(0.001)) + math.log(0.001))
    dt = jnp.maximum(dt, 1e-4)
    dt_bias = dt + jnp.log(-jnp.expm1(-dt))
    return {
        "x": jax.random.normal(ks[0], (BATCH, SEQ, D_MODEL), f32),
        "norm_mix_w": gain(ks[1], (DEPTH, D_MODEL)),
        "w_in": nrm(ks[2], (DEPTH, D_MODEL, IN_COLS), D_MODEL ** -0.5),
        "ssd_conv_w": nrm(ks[3], (DEPTH, SSD_CONV_WIDTH, SSD_CONV_DIM), SSD_CONV_WIDTH ** -0.5),
        "ssd_conv_b": nrm(ks[4], (DEPTH, SSD_CONV_DIM), 0.01),
        "ssd_dt_bias": dt_bias,
        "ssd_a_log": jnp.log(jax.random.uniform(ks[6], (DEPTH, SSD_N_HEADS), f32, 1.0, 16.0)),
        "ssd_d_skip": gain(ks[7], (DEPTH, SSD_N_HEADS)),
        "ssd_norm_w": gain(ks[8], (DEPTH, SSD_D_INNER)),
        "lambda_q1": nrm(ks[9], (DEPTH, ATTN_HEAD_DIM), 0.1),
        "lambda_k1": nrm(ks[10], (DEPTH, ATTN_HEAD_DIM), 0.1),
        "lambda_q2": nrm(ks[11], (DEPTH, ATTN_HEAD_DIM), 0.1),
        "lambda_k2": nrm(ks[12], (DEPTH, ATTN_HEAD_DIM), 0.1),
        "subln_w": gain(ks[13], (DEPTH, 2 * ATTN_HEAD_DIM)),
        "w_branch_ssd": nrm(ks[14], (DEPTH, SSD_D_INNER, D_MODEL), SSD_D_INNER ** -0.5),
        "w_branch_attn": nrm(ks[15], (DEPTH, ATTN_WIDTH, D_MODEL), ATTN_WIDTH ** -0.5),
        "w_out": nrm(ks[16], (DEPTH, D_MODEL, D_MODEL), D_MODEL ** -0.5),
        "norm_ffn_w": gain(ks[17], (DEPTH, D_MODEL)),
        "w_up": nrm(ks[18], (DEPTH, D_MODEL, 2 * D_FF), D_MODEL ** -0.5),
        "ffn_conv_w": nrm(ks[19], (DEPTH, FFN_CONV_WIDTH, D_FF), FFN_CONV_WIDTH ** -0.5),
        "ffn_conv_b": nrm(ks[20], (DEPTH, D_FF), 0.01),
        "w_down": nrm(ks[21], (DEPTH, D_FF, D_MODEL), D_FF ** -0.5),
        "final_norm_w": gain(ks[22], (D_MODEL,)),
    }


def reference(x, norm_mix_w, w_in, ssd_conv_w, ssd_conv_b, ssd_dt_bias, ssd_a_log, ssd_d_skip,
              ssd_norm_w, lambda_q1, lambda_k1, lambda_q2, lambda_k2, subln_w, w_branch_ssd,
              w_branch_attn, w_out, norm_ffn_w, w_up, ffn_conv_w, ffn_conv_b, w_down, final_norm_w):
    split_pts = _split_points(IN_SIZES)
    for i in range(DEPTH):
        lambda_init = 0.8 - 0.6 * math.exp(-0.3 * i)
        h = rms_norm(x, norm_mix_w[i])
        proj = h @ w_in[i]
        z, xbc, dt_raw, q, k, v, g_ssd, g_attn = jnp.split(proj, split_pts, axis=-1)
        y_ssd = ssd_branch(z, xbc, dt_raw, ssd_conv_w[i], ssd_conv_b[i], ssd_dt_bias[i],
                           ssd_a_log[i], ssd_d_skip[i], ssd_norm_w[i])
        y_attn = diff_attn_branch(q, k, v, lambda_q1[i], lambda_k1[i], lambda_q2[i], lambda_k2[i],
                                  subln_w[i], lambda_init)
        merged = (jax.nn.sigmoid(g_ssd) * (y_ssd @ w_branch_ssd[i])
                  + jax.nn.sigmoid(g_attn) * (y_attn @ w_branch_attn[i]))
        x = x + merged @ w_out[i]
        x = x + conv_ffn(rms_norm(x, norm_ffn_w[i]), w_up[i], ffn_conv_w[i], ffn_conv_b[i], w_down[i])
    return rms_norm(x, final_norm_w)
```

```python
import contextlib
import numpy as np
import ml_dtypes
import concourse.bass as bass
import concourse.mybir as mybir
from concourse.bass_utils import run_bass_kernel_spmd

F32 = mybir.dt.float32
BF16 = mybir.dt.bfloat16
AF = mybir.ActivationFunctionType
ALU = mybir.AluOpType
AX = mybir.AxisListType

NCORES = 8
SEQ = 2048
DM = 1024
NT = SEQ // 128
DFF = 2816
NFC = DFF // 128
NORM_EPS = 1e-6
SUBLN_EPS = 1e-5
LAMBDA_INIT = 0.8 - 0.6 * 1.0

U_XS = [0, 1]
U_BC = 2
U_Z = [3, 4]
U_QK = list(range(5, 13))
U_V = [13, 14]
U_GS = [15, 16]
U_GA = [17, 18]
U_BRS = [19, 20]
U_BRA = [21, 22]
U_OUT = [23, 24]
U_UP = list(range(25, 36))
NU1 = 36
U_DN = list(range(36, 42))
NUNITS = 42


class Buf:
    __slots__ = ("name", "w", "r", "excl")

    def __init__(self, name, excl=False):
        self.name = name
        self.w = None
        self.r = []
        self.excl = excl


class Sched:
    def __init__(self, nc, es, needed=None, n_dma_sems=24):
        self.nc = nc
        self.eng = {"pe": nc.tensor, "act": nc.scalar, "dve": nc.vector,
                    "pool": nc.gpsimd, "sp": nc.sync}
        self.sem, self.cnt, self.clock, self.opidx, self.semval = {}, {}, {}, {}, {}
        for e in self.eng:
            self.sem[e] = es.enter_context(nc.semaphore("s_" + e))
            self.cnt[e] = 0
            self.opidx[e] = 0
            self.clock[e] = {}
            self.semval[e] = {}
        self.dsem = [es.enter_context(nc.semaphore("s_dma%d" % i)) for i in range(n_dma_sems)]
        self.dcnt = [0] * n_dma_sems
        self.dlast = [None] * n_dma_sems
        self.dnext = 0
        self.nwaits = 0
        self.ninst = 0
        self.needed_in = needed
        self.needed = set()

    def _wait(self, e, ev):
        key, val, vc = ev
        ck = self.clock[e]
        if ck.get(key, 0) >= val:
            return
        if isinstance(key, str):
            self.needed.add((key, val))
            self.eng[e].wait_ge(self.sem[key], self.semval[key][val])
        else:
            self.eng[e].wait_ge(self.dsem[key], val)
        self.nwaits += 1
        for k, v in vc.items():
            if ck.get(k, 0) < v:
                ck[k] = v

    def _deps(self, e, reads, writes):
        for b in reads:
            if b.w is not None and not (e == "pe" and b.w[0] == "pe"):
                self._wait(e, b.w)
            if b.excl:
                for ev in b.r:
                    if ev[0] != e:
                        self._wait(e, ev)
        for b in writes:
            if b.w is not None and not (e == "pe" and b.w[0] == "pe"):
                self._wait(e, b.w)
            for ev in b.r:
                if ev[0] == e:
                    continue
                self._wait(e, ev)

    def _record(self, ev, reads, writes):
        for b in reads:
            b.r.append(ev)
        for b in writes:
            b.w = ev
            b.r = []

    def _signal(self, e, ins, reads, writes):
        self.opidx[e] += 1
        idx = self.opidx[e]
        if self.needed_in is None or (e, idx) in self.needed_in:
            self.cnt[e] += 1
            ins.then_inc(self.sem[e], 1)
            self.semval[e][idx] = self.cnt[e]
        vc = dict(self.clock[e])
        vc[e] = idx
        ev = (e, idx, vc)
        self._record(ev, reads, writes)
        return ev

    def op(self, e, fn, reads=(), writes=()):
        self._deps(e, reads, writes)
        ins = fn()
        self.ninst += 1
        return self._signal(e, ins, reads, writes)

    def group(self, e, fns, reads=(), writes=()):
        self._deps(e, reads, writes)
        ins = None
        for fn in fns:
            ins = fn()
            self.ninst += 1
        return self._signal(e, ins, reads, writes)

    def partial(self, e, fns, reads=(), writes=()):
        self._deps(e, reads, writes)
        for fn in fns:
            fn()
            self.ninst += 1

    def dma(self, out, in_, reads=(), writes=(), q="sp"):
        i = self.dnext
        self.dnext = (self.dnext + 1) % len(self.dsem)
        if self.dlast[i] is not None:
            self._wait(q, self.dlast[i])
        self._deps(q, reads, writes)
        self.dcnt[i] += 16
        self.eng[q].dma_start(out=out, in_=in_).then_inc(self.dsem[i], 16)
        self.ninst += 1
        vc = dict(self.clock[q])
        vc[i] = self.dcnt[i]
        ev = (i, self.dcnt[i], vc)
        self.dlast[i] = ev
        self._record(ev, reads, writes)
        return ev

    def barrier(self):
        evs = []
        for e in self.eng:
            if self.opidx[e] > 0:
                ck = dict(self.clock[e])
                ck[e] = self.opidx[e]
                evs.append((e, self.opidx[e], ck))
        for ev in self.dlast:
            if ev is not None:
                evs.append(ev)
        for e in self.eng:
            for ev in evs:
                if ev[0] != e:
                    self._wait(e, ev)


class _Stop(Exception):
    pass


def build_program(nseq=4, debug=False, stop_after=None):
    _, S1 = _build(nseq, debug, stop_after, None)
    nc, S2 = _build(nseq, debug, stop_after, S1.needed)
    assert S2.needed == S1.needed
    print("program: %d instructions, %d waits, %d signalling ops" % (S2.ninst, S2.nwaits, sum(S2.cnt.values())))
    return nc


def _build(nseq, debug, stop_after, needed):
    nc = bass.Bass("TRN2", target_bir_lowering=False)

    def din(name, shape, dt=F32):
        return nc.dram_tensor(name, list(shape), dt, kind="ExternalInput").ap()

    x_d = din("x", [nseq, SEQ, DM])
    wcat_d = din("wcat", [DM, NU1 * 512])
    wdn_d = din("wdown", [DFF, DM])
    wdt_d = din("wdt", [128, 8 * 16])
    rowp_d = din("rowp", [1, 5 * 1024])
    rows_d = din("rows", [1, 16 + 16 + 128 + 4 * 64])
    convs_d = din("convs", [128, 12 * 4])
    convf_d = din("convf", [128, NFC * 3])
    cbs_d = din("cbs", [1, 1536])
    cbf_d = din("cbf", [1, DFF])
    cmat_d = din("cmat", [128, 6 * 128])
    onehot_d = din("onehot", [16, 16 * 128], BF16)
    rope_d = din("rope", [128, 2 * SEQ])
    out_d = nc.dram_tensor("out", [nseq, SEQ, DM], F32, kind="ExternalOutput").ap()
    wsc = nc.dram_tensor("wsc", [NUNITS, 128, 4096], BF16, kind="Internal").ap()
    dsc_s = nc.dram_tensor("dsc_s", [128, 12 * 512], BF16, kind="Internal").ap()
    dsc_f = nc.dram_tensor("dsc_f", [128, NFC * 384], BF16, kind="Internal").ap()
    dbg = {}
    if debug:
        for nm, shp, dt in [("d_hT", [128, 8 * SEQ], BF16), ("d_yssdT", [128, 8 * SEQ], BF16),
                            ("d_yattnT", [128, 8 * SEQ], BF16), ("d_mergedT", [128, 8 * SEQ], BF16)]:
            dbg[nm] = nc.dram_tensor(nm, shp, dt, kind="ExternalOutput").ap()

    with contextlib.ExitStack() as es:
        S = Sched(nc, es, needed)
        B_wsc = [Buf("wsc%d" % u) for u in range(NUNITS)]
        B_dsc_s, B_dsc_f = Buf("dsc_s"), Buf("dsc_f")

        uid = [0]

        def alloc(stack, name, shape, dt=F32):
            uid[0] += 1
            t = stack.enter_context(nc.sbuf_tensor("sb_%s_%d" % (name, uid[0]), list(shape), dt))
            return t, Buf(name)

        def palloc(stack, name, shape, dt=F32):
            uid[0] += 1
            t = stack.enter_context(nc.psum_tensor("ps_%s_%d" % (name, uid[0]), list(shape), dt))
            return t, Buf(name, excl=True)

        cmat, b_cmat = alloc(es, "cmat", [128, 6 * 128])
        ident_f = cmat[:, 0:128]
        tri_f = cmat[:, 256:384]
        gt_f = cmat[:, 384:512]
        ones_f = cmat[:, 512:640]
        identb, b_identb = alloc(es, "identb", [128, 128], BF16)
        negmb, b_negmb = alloc(es, "negmb", [128, 128], BF16)
        big0, _ = alloc(es, "big0", [128, 8 * SEQ], BF16)
        rows, b_rows = alloc(es, "rows", [128, 416])
        dtb_bc = rows[:, 0:16]
        alog_bc = rows[:, 16:32]
        subln_bc = rows[:, 32:160]
        small, b_small = alloc(es, "small", [128, 64])
        A_bc = small[:, 0:16]
        neglam = small[:, 16:17]
        neghalf = small[:, 32:48]
        wdtb, b_wdtb = alloc(es, "wdtb", [128, 128], BF16)
        cbs_b, b_cbs = alloc(es, "cbs_b", [1, 1536], BF16)
        cbf_b, b_cbf = alloc(es, "cbf_b", [1, DFF], BF16)
        onesrow, b_onesrow = alloc(es, "onesrow", [1, 512], BF16)
        junk, b_junk = alloc(es, "junk", [128, 1024], BF16)
        onesb, b_onesb = alloc(es, "onesb", [128, 128], BF16)
        S.op("pool", lambda: nc.gpsimd.memset(onesb[:], 1.0), [], [b_onesb])

        S.dma(cmat[:], cmat_d, writes=[b_cmat])
        S.dma(rows[:], rows_d.partition_broadcast(128), writes=[b_rows])
        S.op("dve", lambda: nc.vector.tensor_copy(identb[:], cmat[:, 0:128]), [b_cmat], [b_identb])
        S.op("dve", lambda: nc.vector.tensor_copy(negmb[:], cmat[:, 128:256]), [b_cmat], [b_negmb])
        permb, b_permb = alloc(es, "permb", [128, 128], BF16)
        S.op("dve", lambda: nc.vector.tensor_copy(permb[:], cmat[:, 640:768]), [b_cmat], [b_permb])
        S.op("pool", lambda: nc.gpsimd.memset(small[:, 32:48], -0.5), [], [b_small])
        S.op("pool", lambda: nc.gpsimd.memset(onesrow[:], 1.0), [], [b_onesrow])
        S.op("dve", lambda: nc.vector.tensor_scalar(subln_bc, subln_bc, 1.0 - LAMBDA_INIT, None, ALU.mult),
             [b_rows], [b_rows])
        S.op("act", lambda: nc.scalar.activation(out=A_bc, in_=alog_bc, func=AF.Exp), [b_rows], [b_small])
        S.op("dve", lambda: nc.vector.tensor_scalar(A_bc, A_bc, -1.0, None, ALU.mult), [b_small], [b_small])
        with contextlib.ExitStack() as ph:
            tl, b_tl = alloc(ph, "tl", [128, 128])
            sl, b_sl = alloc(ph, "sl", [128, 4])
            S.op("dve", lambda: nc.vector.tensor_tensor(tl[:, 0:64], rows[:, 160:224], rows[:, 224:288], ALU.mult),
                 [b_rows], [b_tl])
            S.op("dve", lambda: nc.vector.tensor_tensor(tl[:, 64:128], rows[:, 288:352], rows[:, 352:416], ALU.mult),
                 [b_rows], [b_tl])
            S.op("dve", lambda: nc.vector.reduce_sum(sl[:, 0:2], tl[:].rearrange("p (a b) -> p a b", a=2), AX.X),
                 [b_tl], [b_sl])
            S.op("act", lambda: nc.scalar.activation(out=sl[:, 2:4], in_=sl[:, 0:2], func=AF.Exp), [b_sl], [b_sl])
            S.op("dve", lambda: nc.vector.tensor_tensor(sl[:, 0:1], sl[:, 3:4], sl[:, 2:3], ALU.subtract),
                 [b_sl], [b_sl])
            S.op("dve", lambda: nc.vector.tensor_scalar(neglam, sl[:, 0:1], -LAMBDA_INIT, None, ALU.add),
                 [b_sl], [b_small])
            stg, b_stg = alloc(ph, "stg0", [128, DFF])
            S.dma(stg[:, 0:128], wdt_d, writes=[b_stg])
            S.op("dve", lambda: nc.vector.tensor_copy(wdtb[:], stg[:, 0:128]), [b_stg], [b_wdtb])
            S.dma(stg[0:1, 0:1536], cbs_d, writes=[b_stg])
            S.op("dve", lambda: nc.vector.tensor_scalar(cbs_b[:], stg[0:1, 0:1536], 0.5, None, ALU.mult),
                 [b_stg], [b_cbs])
            S.dma(stg[0:1, 0:DFF], cbf_d, writes=[b_stg])
            S.op("dve", lambda: nc.vector.tensor_scalar(cbf_b[:], stg[0:1, 0:DFF], 0.5, None, ALU.mult),
                 [b_stg], [b_cbf])
            cw, b_cw = alloc(ph, "cw", [128, 48 + NFC * 3])
            S.dma(cw[:, 0:48], convs_d, writes=[b_cw])
            S.dma(cw[:, 48:48 + NFC * 3], convf_d, writes=[b_cw])
            dgs, b_dgs = alloc(ph, "dgs", [128, 12 * 512], BF16)
            dgf, b_dgf = alloc(ph, "dgf", [128, NFC * 384], BF16)
            for ch in range(12):
                for j in range(4):
                    eng = "dve" if (ch + j) % 2 == 0 else "pool"
                    e_ = nc.vector if eng == "dve" else nc.gpsimd
                    S.op(eng, lambda e_=e_, ch=ch, j=j: e_.tensor_scalar(
                        dgs[:, ch * 512 + j * 128: ch * 512 + (j + 1) * 128], ident_f,
                        cw[:, ch * 4 + j: ch * 4 + j + 1], 0.5, ALU.mult, ALU.mult), [b_cmat, b_cw], [b_dgs])
            for ch in range(NFC):
                for j in range(3):
                    eng = "dve" if (ch + j) % 2 == 0 else "pool"
                    e_ = nc.vector if eng == "dve" else nc.gpsimd
                    S.op(eng, lambda e_=e_, ch=ch, j=j: e_.tensor_scalar(
                        dgf[:, ch * 384 + j * 128: ch * 384 + (j + 1) * 128], ident_f,
                        cw[:, 48 + ch * 3 + j: 48 + ch * 3 + j + 1], 0.5, ALU.mult, ALU.mult),
                        [b_cmat, b_cw], [b_dgf])
            S.dma(dsc_s, dgs[:], reads=[b_dgs], writes=[B_dsc_s])
            S.dma(dsc_f, dgf[:], reads=[b_dgf], writes=[B_dsc_f])
            S.barrier()

        with contextlib.ExitStack() as ph:
            stgs = [alloc(ph, "wst%d" % i, [128, 4096]) for i in range(3)]
            cvts = [alloc(ph, "wcv%d" % i, [128, 4096], BF16) for i in range(3)]
            for u in range(NUNITS if stop_after != "const" else 0):
                st, b_st = stgs[u % 3]
                cv, b_cv = cvts[u % 3]
                if u < NU1:
                    src = wcat_d[:, u * 512:(u + 1) * 512].rearrange("(k p) c -> p k c", p=128)
                    S.dma(st[:].rearrange("p (k c) -> p k c", k=8), src, writes=[b_st])
                    nk = 8
                else:
                    half, ku = divmod(u - NU1, 3)
                    nk = 8 if ku < 2 else 6
                    src = wdn_d[ku * 1024: ku * 1024 + nk * 128, half * 512:(half + 1) * 512].rearrange(
                        "(k p) c -> p k c", p=128)
                    S.dma(st[:, 0:nk * 512].rearrange("p (k c) -> p k c", k=nk), src, writes=[b_st])
                n = nk * 512
                h1 = n // 2
                S.op("dve", lambda cv=cv, st=st, h1=h1: nc.vector.tensor_copy(cv[:, 0:h1], st[:, 0:h1]),
                     [b_st], [b_cv])
                S.op("act", lambda cv=cv, st=st, h1=h1, n=n: nc.scalar.copy(cv[:, h1:n], st[:, h1:n]),
                     [b_st], [b_cv])
                S.dma(wsc[u, :, 0:n], cv[:, 0:n], reads=[b_cv], writes=[B_wsc[u]], q="act")
            S.barrier()

        class Ring:
            def __init__(self, stack, n, tag, view=None):
                if view is None:
                    self.slots = [alloc(stack, "ring%s%d" % (tag, i), [128, 4096], BF16) for i in range(n)]
                    self.slots = [(t[:], b) for t, b in self.slots]
                else:
                    self.slots = [(view[:, i * 4096:(i + 1) * 4096], Buf("ringv%d" % i)) for i in range(n)]
                self.i = 0

            def load(self, u, n=4096):
                t, b = self.slots[self.i]
                self.i = (self.i + 1) % len(self.slots)
                S.dma(t[:, 0:n], wsc[u, :, 0:n], reads=[B_wsc[u]], writes=[b])
                return t, b

        def rms_to_T(src, b_src, w_bc, b_w, dstT, b_dst, col0, hb, b_hb, st2, b_st2, psT, b_psT, evac_eng):
            import os
            lvl = int(os.environ.get("K_DBG_A", "9"))
            if lvl < 2:
                return
            S.op("act", lambda: nc.scalar.activation(out=junk[:], in_=src, func=AF.Square, accum_out=st2[:, 0:1]),
                 [b_src], [b_junk, b_st2])
            if lvl < 3:
                return
            S.op("dve", lambda: nc.vector.tensor_scalar(st2[:, 1:2], st2[:, 0:1], 1.0 / DM, NORM_EPS,
                                                        ALU.mult, ALU.add), [b_st2], [b_st2])
            S.op("pool", lambda: nc.gpsimd.tensor_tensor(st2[:, 2:3], st2[:, 1:2], neghalf[:, 0:1], ALU.pow),
                 [b_st2, b_small], [b_st2])
            if lvl < 4:
                return
            S.op("dve", lambda: nc.vector.scalar_tensor_tensor(hb[:], src, st2[:, 2:3], w_bc, ALU.mult, ALU.mult),
                 [b_src, b_st2, b_w], [b_hb])
            if lvl < 5:
                return
            S.group("pe", [lambda kc=kc: nc.tensor.transpose(psT[:, kc * 128:(kc + 1) * 128],
                                                            hb[:, kc * 128:(kc + 1) * 128], identb[:])
                           for kc in range(8)], [b_hb, b_identb], [b_psT])
            if lvl < 6:
                return
            o_ap = dstT[:, :, col0:col0 + 128]
            i_ap = psT[:, :].rearrange("p (k j) -> p k j", k=8)
            if evac_eng == "act":
                S.op("act", lambda: nc.scalar.copy(o_ap, i_ap), [b_psT], [b_dst])
            else:
                S.op("dve", lambda: nc.vector.tensor_copy(o_ap, i_ap), [b_psT], [b_dst])

        out_evs = []

        for s in range(nseq if stop_after not in ("const", "pre") else 0):
          try:
            so = contextlib.ExitStack()
            mergedT = big0[:].rearrange("p (k t) -> p k t", k=8)
            b_mergedT = Buf("mergedT")
            sq = contextlib.ExitStack()
            if True:
                hT, _ = alloc(sq, "hT", [128, 8, SEQ], BF16)
                b_hT = [Buf("hT%d" % t) for t in range(NT)]
                yssdT, _ = alloc(sq, "yssdT", [128, 8, SEQ], BF16)
                b_yssdT = [Buf("yssdT%d" % t) for t in range(NT)]

                with contextlib.ExitStack() as ph:
                    xts = [alloc(ph, "xa%d" % i, [128, DM]) for i in range(4)]
                    hbs = [alloc(ph, "hba%d" % i, [128, DM], BF16) for i in range(2)]
                    st2s = [alloc(ph, "sta%d" % i, [128, 4]) for i in range(2)]
                    psTs = [palloc(ph, "psTa%d" % i, [128, 1024], BF16) for i in range(2)]
                    wmix, b_rowp = alloc(ph, "wmix", [128, DM])
                    wmix_bc = wmix[:]
                    S.dma(wmix[:], rowp_d[:, 0:1024].partition_broadcast(128), writes=[b_rowp])
                    for t in range(NT):
                        xt, b_xt = xts[t % 4]
                        S.dma(xt[:], x_d[s, t * 128:(t + 1) * 128, :], writes=[b_xt])
                        rms_to_T(xt[:], b_xt, wmix_bc, b_rowp, hT, b_hT[t], t * 128, hbs[t % 2][0], hbs[t % 2][1],
                                 st2s[t % 2][0], st2s[t % 2][1], psTs[t % 2][0], psTs[t % 2][1],
                                 "act" if t % 2 == 0 else "dve")
                    S.barrier()
                if debug and s == 0:
                    out_evs.append(S.dma(dbg["d_hT"], hT[:].rearrange("p k t -> p (k t)"), reads=b_hT))

                if stop_after == "A":
                    raise _Stop()
                with contextlib.ExitStack() as ph:
                    ring = Ring(ph, 4, "b", view=big0)
                    rowb, b_rowp = alloc(ph, "rowb", [128, 2048])
                    S.dma(rowb[:], rowp_d[:, 3072:5120].partition_broadcast(128), writes=[b_rowp])
                    ssdnw_bc = rowb[:, 0:1024]
                    drep_bc = rowb[:, 1024:2048]
                    onehot, b_onehot = alloc(ph, "onehot", [16, 16 * 128], BF16)
                    S.dma(onehot[:], onehot_d, writes=[b_onehot])
                    dgs, b_dgs = alloc(ph, "dgsb", [128, 12 * 512], BF16)
                    S.dma(dgs[:], dsc_s, reads=[B_dsc_s], writes=[b_dgs])
                    cin, b_cin = alloc(ph, "cin", [128, 3 + SEQ], BF16)
                    cout, b_cout = alloc(ph, "cout", [128, SEQ], BF16)
                    BT, b_BT = alloc(ph, "BT", [128, SEQ], BF16)
                    CT, b_CT = alloc(ph, "CT", [128, SEQ], BF16)
                    xs_tok, b_xs_tok = alloc(ph, "xs_tok", [128, NT, 512], BF16)
                    Btok, b_Btok = alloc(ph, "Btok", [128, NT, 128], BF16)
                    dtt, b_dtt = alloc(ph, "dtt", [128, NT, 16])
                    dA, b_dA = alloc(ph, "dA", [128, NT, 16])
                    ex, b_ex = alloc(ph, "ex", [128, NT, 48])
                    nacs, b_nacs = alloc(ph, "nacs", [128, NT, 16])
                    acsTs = [alloc(ph, "acsT%d" % i, [16, 128]) for i in range(2)]
                    acHs = [alloc(ph, "acH%d" % i, [16, 128], BF16) for i in range(2)]
                    acLs = [alloc(ph, "acL%d" % i, [16, 128], BF16) for i in range(2)]
                    cbT, b_cbT = alloc(ph, "cbT", [128, 128])
                    Es = [alloc(ph, "E%d" % i, [128, 128]) for i in range(2)]
                    MTs = [alloc(ph, "MT%d" % i, [128, 128], BF16) for i in range(16)]
                    Xcs = [alloc(ph, "Xc%d" % i, [128, 512], BF16) for i in range(2)]
                    Xds = [alloc(ph, "Xd%d" % i, [128, 512], BF16) for i in range(2)]
                    Sf, b_Sf = alloc(ph, "Sf", [128, 512])
                    Sb, b_Sb = alloc(ph, "Sb", [128, 512], BF16)
                    y1, b_y1 = alloc(ph, "y1", [128, 512])
                    cout_f = cout[:].bitcast(F32)
                    y2s = [(cout_f[:, i * 512:(i + 1) * 512], Buf("y2_%d" % i)) for i in range(2)]
                    y2, b_y2 = y2s[0]
                    sk, b_sk = y1, b_y1
                    zz, b_zz = alloc(ph, "zz", [128, 512])
                    ygns = [alloc(ph, "ygn%d" % i, [128, 512], BF16) for i in range(2)]
                    th, b_th = alloc(ph, "th", [128, 512])
                    tz, b_tz = th, b_th
                    tmpa, b_tmpa = th[:, 0:256], b_th
                    tmpb, b_tmpb = y1[:, 0:256], b_y1
                    tmpc, b_tmpc = y2[:, 0:256], b_y2
                    stb, b_stb = alloc(ph, "stb", [128, 4])
                    pA, b_pA = palloc(ph, "pA", [128, 512])
                    pB, b_pB = palloc(ph, "pB", [128, 512])
                    pC, b_pC = palloc(ph, "pC", [128, 512])
                    pD, b_pD = palloc(ph, "pD", [128, 512])
                    pE, b_pE = palloc(ph, "pE", [128, 512])
                    pF, b_pF = palloc(ph, "pF", [128, 512])
                    pT, b_pT = palloc(ph, "pTb", [128, 1024], BF16)
                    pT2, b_pT2 = palloc(ph, "pTb2", [128, 1024], BF16)

                    S.op("pool", lambda: nc.gpsimd.memset(cin[:, 0:3], 0.0), [], [b_cin])

                    S.group("pe", [lambda c=c, kc=kc: nc.tensor.matmul(
                        pA[:, c * 16:(c + 1) * 16], hT[:, kc, c * 128:(c + 1) * 128], wdtb[:, kc * 16:(kc + 1) * 16],
                        start=(kc == 0), stop=(kc == 7), skip_group_check=True)
                        for c in range(NT) for kc in range(8)], b_hT + [b_wdtb], [b_pA])
                    S.op("dve", lambda: nc.vector.tensor_tensor(
                        tmpa.rearrange("p (c h) -> p c h", h=16), pA[:, 0:256].rearrange("p (c h) -> p c h", h=16),
                        dtb_bc.unsqueeze(1).to_broadcast([128, NT, 16]), ALU.add), [b_pA, b_rows], [b_tmpa])
                    S.op("act", lambda: nc.scalar.activation(out=tmpb, in_=tmpa, func=AF.Abs), [b_tmpa], [b_tmpb])
                    S.op("act", lambda: nc.scalar.activation(out=tmpb, in_=tmpb, func=AF.Exp, scale=-1.0),
                         [b_tmpb], [b_tmpb])
                    S.op("act", lambda: nc.scalar.activation(out=tmpc, in_=tmpb, func=AF.Ln, bias=1.0),
                         [b_tmpb], [b_tmpc])
                    S.op("dve", lambda: nc.vector.scalar_tensor_tensor(
                        dtt[:].rearrange("p c h -> p (c h)"), tmpa, 0.0, tmpc, ALU.max, ALU.add),
                        [b_tmpa, b_tmpc], [b_dtt])
                    S.op("dve", lambda: nc.vector.tensor_tensor(
                        dA[:], dtt[:], A_bc.unsqueeze(1).to_broadcast([128, NT, 16]), ALU.mult),
                        [b_dtt, b_small], [b_dA])
                    for half in range(2):
                        pX, b_pX = (pB, b_pB) if half == 0 else (pC, b_pC)
                        fns = []
                        for cc in range(8):
                            c = half * 8 + cc
                            for k3, lh in enumerate((tri_f, gt_f, ones_f)):
                                fns.append(lambda c=c, cc=cc, k3=k3, lh=lh: nc.tensor.matmul(
                                    pX[:, cc * 48 + k3 * 16: cc * 48 + (k3 + 1) * 16], lh, dA[:, c, :],
                                    start=True, stop=True, skip_group_check=True))
                        S.group("pe", fns, [b_dA, b_cmat], [b_pX])
                        S.op("act", lambda pX=pX, half=half: nc.scalar.activation(
                            out=ex[:, half * 8:(half + 1) * 8, :],
                            in_=pX[:, 0:384].rearrange("p (c k) -> p c k", k=48), func=AF.Exp), [b_pX], [b_ex])
                        S.op("dve", lambda pX=pX, half=half: nc.vector.tensor_scalar(
                            nacs[:, half * 8:(half + 1) * 8, :],
                            pX[:, 0:384].rearrange("p (c k) -> p c k", k=48)[:, :, 0:16], -1.0, None, ALU.mult),
                            [b_pX], [b_nacs])
                    if stop_after == "B0":
                        S.barrier()
                        ph.close()
                        raise _Stop()

                    for g in range(2):
                        if g > 0:
                            S.barrier()
                        uXS, b_uXS = ring.load(U_XS[g])
                        uBC, b_uBC = ring.load(U_BC)
                        uZ, b_uZ = ring.load(U_Z[g])
                        chans = [("xs", i) for i in range(4)] + [("B", 0), ("C", 0)]
                        for kind, i in chans:
                            if kind == "xs":
                                unit, b_unit, cb0, ch = uXS, b_uXS, i * 128, g * 4 + i
                            elif kind == "B":
                                unit, b_unit, cb0, ch = uBC, b_uBC, g * 256, 8 + g
                            else:
                                unit, b_unit, cb0, ch = uBC, b_uBC, g * 256 + 128, 10 + g
                            for tg in range(4):
                                pX, b_pX = (pA, b_pA) if tg % 2 == 0 else (pB, b_pB)
                                S.group("pe", [lambda kc=kc, pX=pX, unit=unit, cb0=cb0, tg=tg: nc.tensor.matmul(
                                    pX[:, :], unit[:, kc * 512 + cb0: kc * 512 + cb0 + 128],
                                    hT[:, kc, tg * 512:(tg + 1) * 512], start=(kc == 0), stop=(kc == 7))
                                    for kc in range(8)], b_hT[tg * 4:(tg + 1) * 4] + [b_unit], [b_pX])
                                S.op("act", lambda pX=pX, tg=tg: nc.scalar.copy(
                                    cin[:, 3 + tg * 512: 3 + (tg + 1) * 512], pX[:, :]), [b_pX], [b_cin])
                            for tg in range(4):
                                pX, b_pX = (pC, b_pC) if tg % 2 == 0 else (pD, b_pD)
                                fns = [lambda j=j, pX=pX, ch=ch, tg=tg: nc.tensor.matmul(
                                    pX[:, :], dgs[:, ch * 512 + j * 128: ch * 512 + (j + 1) * 128],
                                    cin[:, tg * 512 + j: tg * 512 + j + 512], start=(j == 0), stop=False)
                                    for j in range(4)]
                                fns.append(lambda pX=pX, ch=ch: nc.tensor.matmul(
                                    pX[:, :], cbs_b[0:1, ch * 128:(ch + 1) * 128], onesrow[0:1, :],
                                    start=False, stop=True))
                                S.group("pe", fns, [b_cin, b_dgs, b_cbs, b_onesrow], [b_pX])
                                S.op("act", lambda pX=pX: nc.scalar.activation(out=th[:], in_=pX[:, :], func=AF.Tanh),
                                     [b_pX], [b_th])
                                dst, b_dst = (cout, b_cout) if kind == "xs" else ((BT, b_BT) if kind == "B" else (CT, b_CT))
                                S.op("dve", lambda pX=pX, dst=dst, tg=tg: nc.vector.scalar_tensor_tensor(
                                    dst[:, tg * 512:(tg + 1) * 512], th[:], 1.0, pX[:, :], ALU.add, ALU.mult),
                                    [b_th, b_pX], [b_dst])
                            if kind in ("xs", "B"):
                                src, b_src = (cout, b_cout) if kind == "xs" else (BT, b_BT)
                                for hh in range(2):
                                    pX, b_pX = (pT, b_pT) if hh == 0 else (pT2, b_pT2)
                                    S.group("pe", [lambda t=t, pX=pX, src=src, hh=hh: nc.tensor.transpose(
                                        pX[:, t * 128:(t + 1) * 128], src[:, (hh * 8 + t) * 128:(hh * 8 + t + 1) * 128],
                                        identb[:]) for t in range(8)], [b_src, b_identb], [b_pX])
                                    if kind == "xs":
                                        o_ap = xs_tok[:, hh * 8:(hh + 1) * 8, i * 128:(i + 1) * 128]
                                        b_o = b_xs_tok
                                    else:
                                        o_ap = Btok[:, hh * 8:(hh + 1) * 8, :]
                                        b_o = b_Btok
                                    i_ap = pX[:, :].rearrange("p (t j) -> p t j", t=8)
                                    if hh == 0:
                                        S.op("act", lambda o_ap=o_ap, i_ap=i_ap: nc.scalar.copy(o_ap, i_ap), [b_pX], [b_o])
                                    else:
                                        S.op("dve", lambda o_ap=o_ap, i_ap=i_ap: nc.vector.tensor_copy(o_ap, i_ap),
                                             [b_pX], [b_o])

                        if stop_after == "B1":
                            S.barrier()
                            ph.close()
                            raise _Stop()
                        def front_pre(c):
                            csl = slice(c * 128, (c + 1) * 128)
                            acsT, b_acsT = acsTs[c % 2]
                            S.group("pe", [lambda: nc.tensor.matmul(pA[0:16, 0:128], dA[:, c, :], tri_f,
                                                                    start=True, stop=True)],
                                    [b_dA, b_cmat], [b_pA])
                            S.op("dve", lambda: nc.vector.tensor_copy(acsT[:], pA[0:16, 0:128]), [b_pA], [b_acsT])
                            acH, b_acH = acHs[c % 2]
                            acL, b_acL = acLs[c % 2]
                            S.op("dve", lambda: nc.vector.tensor_copy(acH[:], acsT[:]), [b_acsT], [b_acH])
                            S.op("dve", lambda: nc.vector.tensor_tensor(acL[:], acsT[:], acH[:], ALU.subtract),
                                 [b_acsT, b_acH], [b_acL])
                            S.group("pe", [lambda: nc.tensor.matmul(pB[:, 0:128], BT[:, csl], CT[:, csl],
                                                                    start=True, stop=True)], [b_BT, b_CT], [b_pB])
                            S.op("dve", lambda: nc.vector.tensor_copy(cbT[:], pB[:, 0:128]), [b_pB], [b_cbT])
                            Xc, b_Xc = Xcs[c % 2]
                            Xd, b_Xd = Xds[c % 2]
                            S.op("dve", lambda: nc.vector.tensor_tensor(
                                Xc[:].rearrange("p (h d) -> p h d", h=8),
                                xs_tok[:, c, :].rearrange("p (h d) -> p h d", h=8),
                                dtt[:, c, g * 8:(g + 1) * 8].unsqueeze(2).to_broadcast([128, 8, 64]), ALU.mult),
                                [b_xs_tok, b_dtt], [b_Xc])
                            S.op("pool", lambda: nc.gpsimd.tensor_tensor(
                                Xd[:].rearrange("p (h d) -> p h d", h=8), Xc[:].rearrange("p (h d) -> p h d", h=8),
                                ex[:, c, 16 + g * 8: 16 + (g + 1) * 8].unsqueeze(2).to_broadcast([128, 8, 64]),
                                ALU.mult), [b_Xc, b_ex], [b_Xd])

                        def front_head(c, h):
                            hh = g * 8 + h
                            acsT, b_acsT = acsTs[c % 2]
                            pX, b_pX = (pC, b_pC) if h % 2 == 0 else (pD, b_pD)
                            E, b_E = Es[h % 2]
                            MT, b_MT = MTs[(c % 2) * 8 + h]
                            acH, b_acH = acHs[c % 2]
                            acL, b_acL = acLs[c % 2]
                            S.group("pe", [
                                lambda: nc.tensor.matmul(pX[:, 0:128], onehot[0:16, hh * 128:(hh + 1) * 128], acH[:],
                                                         start=True, stop=False),
                                lambda: nc.tensor.matmul(pX[:, 0:128], onehot[0:16, hh * 128:(hh + 1) * 128], acL[:],
                                                         start=False, stop=False),
                                lambda: nc.tensor.matmul(pX[:, 0:128], identb[:], negmb[:], start=False, stop=True)],
                                [b_onehot, b_acH, b_acL, b_identb, b_negmb], [b_pX])
                            S.op("act", lambda: nc.scalar.activation(
                                out=E[:], in_=pX[:, 0:128], func=AF.Exp, bias=nacs[:, c, hh:hh + 1]),
                                [b_pX, b_nacs], [b_E])
                            S.op("pool", lambda: nc.gpsimd.tensor_tensor(MT[:], E[:], cbT[:], ALU.mult),
                                 [b_E, b_cbT], [b_MT])

                        def mid(c):
                            csl = slice(c * 128, (c + 1) * 128)
                            y2, b_y2 = y2s[c % 2]
                            Xc, b_Xc = Xcs[c % 2]
                            Xd, b_Xd = Xds[c % 2]
                            S.group("pe", [lambda h=h: nc.tensor.matmul(
                                pE[:, h * 64:(h + 1) * 64], MTs[(c % 2) * 8 + h][0][:], Xc[:, h * 64:(h + 1) * 64],
                                start=(h == 0), stop=(h == 7), skip_group_check=True) for h in range(8)],
                                [MTs[(c % 2) * 8 + h][1] for h in range(8)] + [b_Xc], [b_pE])
                            if c > 0:
                                S.group("pe", [lambda: nc.tensor.matmul(pF[:, :], CT[:, csl], Sb[:],
                                                                        start=True, stop=True)],
                                        [b_CT, b_Sb], [b_pF])
                                S.op("dve", lambda: nc.vector.tensor_tensor(
                                    y1[:].rearrange("p (h d) -> p h d", h=8),
                                    pF[:, :].rearrange("p (h d) -> p h d", h=8),
                                    ex[:, c, g * 8:(g + 1) * 8].unsqueeze(2).to_broadcast([128, 8, 64]), ALU.mult),
                                    [b_pF, b_ex], [b_y1])
                                S.op("dve", lambda: nc.vector.tensor_tensor(y2[:], pE[:, :], y1[:], ALU.add),
                                     [b_pE, b_y1], [b_y2])
                            else:
                                S.op("dve", lambda: nc.vector.tensor_copy(y2[:], pE[:, :]), [b_pE], [b_y2])
                            if c < NT - 1:
                                S.group("pe", [lambda: nc.tensor.matmul(pB[:, :], Btok[:, c, :], Xd[:],
                                                                        start=True, stop=True)],
                                        [b_Btok, b_Xd], [b_pB])
                                if c == 0:
                                    S.op("dve", lambda: nc.vector.tensor_copy(Sf[:], pB[:, :]), [b_pB], [b_Sf])
                                else:
                                    S.op("dve", lambda: nc.vector.tensor_tensor(
                                        Sf[:].rearrange("p (h d) -> p h d", h=8), Sf[:].rearrange("p (h d) -> p h d", h=8),
                                        ex[:, c, 32 + g * 8: 32 + (g + 1) * 8].unsqueeze(2).to_broadcast([128, 8, 64]),
                                        ALU.mult), [b_Sf, b_ex], [b_Sf])
                                    S.op("dve", lambda: nc.vector.tensor_tensor(Sf[:], Sf[:], pB[:, :], ALU.add),
                                         [b_Sf, b_pB], [b_Sf])
                                S.op("act", lambda: nc.scalar.copy(Sb[:], Sf[:]), [b_Sf], [b_Sb])

                        def tail_ops(c):
                            csl = slice(c * 128, (c + 1) * 128)
                            y2, b_y2 = y2s[c % 2]
                            ygn, b_ygn = ygns[c % 2]
                            ops = []
                            ops.append(lambda: S.op("pool", lambda: nc.gpsimd.tensor_tensor(
                                sk[:], xs_tok[:, c, :], drep_bc[:, g * 512:(g + 1) * 512], ALU.mult),
                                [b_xs_tok, b_rowp], [b_sk]))
                            ops.append(lambda: S.group("pe", [lambda kc=kc: nc.tensor.matmul(
                                pA[:, :], hT[:, kc, csl], uZ[:, kc * 512:(kc + 1) * 512],
                                start=(kc == 0), stop=(kc == 7)) for kc in range(8)], [b_hT[c], b_uZ], [b_pA]))
                            ops.append(lambda: S.op("dve", lambda: nc.vector.tensor_tensor(y2[:], y2[:], sk[:], ALU.add),
                                                    [b_y2, b_sk], [b_y2]))
                            ops.append(lambda: S.op("act", lambda: nc.scalar.activation(
                                out=tz[:], in_=pA[:, :], func=AF.Tanh, scale=0.5), [b_pA], [b_tz]))
                            ops.append(lambda: S.op("dve", lambda: nc.vector.scalar_tensor_tensor(
                                zz[:], tz[:], 1.0, pA[:, :], ALU.add, ALU.mult), [b_tz, b_pA], [b_zz]))
                            ops.append(lambda: S.op("dve", lambda: nc.vector.tensor_tensor(y2[:], y2[:], zz[:], ALU.mult),
                                                    [b_y2, b_zz], [b_y2]))
                            ops.append(lambda: S.op("act", lambda: nc.scalar.activation(
                                out=junk[:, 0:512], in_=y2[:], func=AF.Square, accum_out=stb[:, 0:1]),
                                [b_y2], [b_junk, b_stb]))

                            def stats():
                                S.op("dve", lambda: nc.vector.tensor_scalar(stb[:, 1:2], stb[:, 0:1], 1.0 / 512,
                                                                            4.0 * NORM_EPS, ALU.mult, ALU.add),
                                     [b_stb], [b_stb])
                                S.op("pool", lambda: nc.gpsimd.tensor_tensor(stb[:, 2:3], stb[:, 1:2], neghalf[:, 0:1],
                                                                             ALU.pow), [b_stb, b_small], [b_stb])
                                S.op("dve", lambda: nc.vector.scalar_tensor_tensor(
                                    ygn[:], y2[:], stb[:, 2:3], ssdnw_bc[:, g * 512:(g + 1) * 512], ALU.mult, ALU.mult),
                                    [b_y2, b_stb, b_rowp], [b_ygn])

                            def fin():
                                S.group("pe", [lambda j=j: nc.tensor.transpose(
                                    pT[:, j * 128:(j + 1) * 128], ygn[:, j * 128:(j + 1) * 128], identb[:])
                                    for j in range(4)], [b_ygn, b_identb], [b_pT])
                                S.op("act", lambda: nc.scalar.copy(
                                    yssdT[:, g * 4:(g + 1) * 4, csl], pT[:, 0:512].rearrange("p (k j) -> p k j", k=4)),
                                    [b_pT], [b_yssdT[c]])
                            early = ops[0:6]
                            late = [ops[6], stats, fin]
                            return early, late

                        front_pre(0)
                        for h in range(8):
                            front_head(0, h)
                        late_prev = []
                        for c in range(NT):
                            if c + 1 < NT:
                                front_pre(c + 1)
                            mid(c)
                            early, late = tail_ops(c)
                            todo = late_prev + early
                            if c + 1 < NT:
                                for h in range(8):
                                    front_head(c + 1, h)
                                    if todo:
                                        todo.pop(0)()
                            while todo:
                                todo.pop(0)()
                            late_prev = late
                        for o_ in late_prev:
                            o_()
                    S.barrier()
                if debug and s == 0:
                    out_evs.append(S.dma(dbg["d_yssdT"], yssdT[:].rearrange("p k t -> p (k t)"), reads=b_yssdT))

                if stop_after == "B":
                    raise _Stop()
                yattnT, _ = alloc(sq, "yattnT", [128, 8, SEQ], BF16)
                b_yattnT = [Buf("yattnT%d" % t) for t in range(NT)]

                with contextlib.ExitStack() as ph:
                    ring = Ring(ph, 4, "c", view=big0)
                    rope, b_rope = alloc(ph, "rope", [128, 2 * SEQ])
                    S.dma(rope[:], rope_d, writes=[b_rope])
                    QTs = [alloc(ph, "QT%d" % i, [128, SEQ], BF16) for i in range(1)] * 2
                    KTs = [alloc(ph, "KT%d" % i, [128, SEQ], BF16) for i in range(1)] * 2
                    Vaugs = [alloc(ph, "Vaug%d" % i, [128, NT, 128], BF16) for i in range(1)] * 2
                    t1s = [alloc(ph, "t1_%d" % i, [128, 512]) for i in range(2)]
                    t2s = [alloc(ph, "t2_%d" % i, [128, 512]) for i in range(2)]
                    pTs_ = [[alloc(ph, "pT%d%d" % (i, j), [128, 512], BF16) for j in range(2)] for i in range(2)]
                    qbs = [alloc(ph, "qb%d" % i, [128, 512], BF16) for i in range(2)]
                    o1, b_o1 = alloc(ph, "o1", [128, 512])
                    o2, b_o2 = alloc(ph, "o2", [128, 512])
                    rr, b_rr = alloc(ph, "rr", [128, 512])
                    rsums = [alloc(ph, "rsum%d" % i, [128, 512]) for i in range(2)]
                    rsbs = [alloc(ph, "rsb%d" % i, [128, 512], BF16) for i in range(2)]
                    nh384, b_nh384 = alloc(ph, "epsc", [128, 2])
                    S.op("pool", lambda: nc.gpsimd.memset(nh384[:], SUBLN_EPS), [], [b_nh384])
                    sublncol, b_sublncol = alloc(ph, "sublncol", [128, 2])
                    S.dma(sublncol[:, 0:1], rows_d[0:1, 32:160].rearrange("o e -> e o"), writes=[b_sublncol])
                    S.op("dve", lambda: nc.vector.tensor_scalar(sublncol[:, 0:1], sublncol[:, 0:1], 1.0 - LAMBDA_INIT,
                                                                None, ALU.mult), [b_sublncol], [b_sublncol])
                    pN, b_pN = palloc(ph, "pN", [128, 512])
                    pSt = [palloc(ph, "pSt%d" % i, [128, 512]) for i in range(2)]
                    pAccs = [[palloc(ph, "pAcc%d%d" % (j, i), [128, 512]) for i in range(2)] for j in range(2)]
                    pJ = [pSt[0], pSt[1], palloc(ph, "pJ2", [128, 512]), (pN, b_pN), pAccs[0][0], pAccs[0][1]]
                    qg_i = [0]
                    qgroups = [(0, 4), (4, 4), (8, 4), (12, 4)]
                    pj_i = [0]

                    def next_pj():
                        r = pJ[pj_i[0] % 6]
                        pj_i[0] += 1
                        return r

                    def proj_stream(h):
                        QT, b_QT = QTs[0]
                        KT, b_KT = KTs[0]
                        Vaug, b_Vaug = Vaugs[0]
                        uQK, b_uQK = ring.load(U_QK[h])
                        cnt = 0
                        for tg in range(4):
                            tsl = slice(tg * 512, (tg + 1) * 512)
                            pas, qb_l = [], []
                            for which in range(2):
                                pa, b_pa = next_pj()
                                S.group("pe", [lambda kc=kc: nc.tensor.matmul(
                                    pa[:, :], uQK[:, kc * 512 + which * 128: kc * 512 + (which + 1) * 128],
                                    hT[:, kc, tsl], start=(kc == 0), stop=(kc == 7)) for kc in range(8)],
                                    b_hT[tg * 4:(tg + 1) * 4] + [b_uQK], [b_pa])
                                qb, b_qb = qbs[which]
                                S.op("act", lambda: nc.scalar.copy(qb[:], pa[:, :]), [b_pa], [b_qb])
                                pas.append((pa, b_pa))
                                qb_l.append((qb, b_qb))
                            pV, b_pV = next_pj()
                            for t in range(4):
                                S.group("pe", [lambda kc=kc: nc.tensor.matmul(
                                    pV[:, t * 128:(t + 1) * 128], hT[:, kc, (tg * 4 + t) * 128:(tg * 4 + t + 1) * 128],
                                    uQK[:, kc * 512 + 256: kc * 512 + 384],
                                    start=(kc == 0), stop=(kc == 7), skip_group_check=True) for kc in range(8)],
                                    [b_hT[tg * 4 + t], b_uQK], [b_pV])
                            S.op("act", lambda: nc.scalar.copy(
                                Vaug[:, tg * 4:(tg + 1) * 4, :], pV[:, :].rearrange("p (t e) -> p t e", t=4)),
                                [b_pV], [b_Vaug])
                            for which, (dst, b_dst) in enumerate(((QT, b_QT), (KT, b_KT))):
                                pa, b_pa = pas[which]
                                qb, b_qb = qb_l[which]
                                t1, b_t1 = t1s[which]
                                t2, b_t2 = t2s[which]
                                pr, b_pr = next_pj()
                                S.group("pe", [lambda: nc.tensor.matmul(pr[:, :], permb[:], qb[:], start=True, stop=True)],
                                        [b_permb, b_qb], [b_pr])
                                S.op("dve", lambda: nc.vector.tensor_tensor(
                                    t1[:], pa[:, :], rope[:, tg * 512:(tg + 1) * 512], ALU.mult),
                                    [b_pa, b_rope], [b_t1])
                                S.op("dve", lambda: nc.vector.tensor_tensor(
                                    t2[:], pr[:, :], rope[:, SEQ + tg * 512: SEQ + (tg + 1) * 512], ALU.mult),
                                    [b_pr, b_rope], [b_t2])
                                S.op("pool", lambda: nc.gpsimd.tensor_tensor(dst[:, tsl], t1[:], t2[:], ALU.add),
                                     [b_t1, b_t2], [b_dst])
                            yield

                    pending = []

                    def attn_stream(h):
                        QT, b_QT = QTs[0]
                        KT, b_KT = KTs[0]
                        Vaug, b_Vaug = Vaugs[0]
                        for (q0, nq) in qgroups:
                            nk = q0 + nq
                            W = nq * 128
                            pAcc = pAccs[qg_i[0] % 2]
                            qg_i[0] += 1

                            def emit_st(kt, i):
                                lo = max(0, kt - q0)
                                pX, b_pX = pSt[i]
                                fns = [lambda: nc.tensor.matmul(
                                    pX[:, lo * 128: W], KT[i * 64:(i + 1) * 64, kt * 128:(kt + 1) * 128],
                                    QT[i * 64:(i + 1) * 64, (q0 + lo) * 128:(q0 + nq) * 128],
                                    start=True, stop=(kt < q0))]
                                if kt >= q0:
                                    fns.append(lambda: nc.tensor.matmul(
                                        pX[:, lo * 128:(lo + 1) * 128], identb[:], negmb[:], start=False, stop=True))
                                S.group("pe", fns, [b_KT, b_QT, b_identb, b_negmb], [b_pX])

                            emit_st(0, 0)
                            emit_st(0, 1)
                            for kt in range(nk):
                                c0 = max(0, kt - q0) * 128
                                pts = []
                                for i in range(2):
                                    pX, b_pX = pSt[i]
                                    pTt_, b_pTt_ = pTs_[i][kt % 2]
                                    pts.append((pTt_, b_pTt_))
                                    S.op("act", lambda: nc.scalar.activation(
                                        out=pTt_[:, c0:W], in_=pX[:, c0:W], func=AF.Exp, scale=0.125), [b_pX], [b_pTt_])
                                    if kt + 1 < nk:
                                        emit_st(kt + 1, i)
                                    rsum, b_rsum = rsums[i]
                                    eng, e_ = ("dve", nc.vector) if i == 0 else ("pool", nc.gpsimd)
                                    if kt == 0:
                                        S.op(eng, lambda: e_.tensor_copy(rsum[:, 0:W], pTt_[:, 0:W]), [b_pTt_], [b_rsum])
                                    else:
                                        S.op(eng, lambda: e_.tensor_tensor(rsum[:, c0:W], rsum[:, c0:W], pTt_[:, c0:W],
                                                                           ALU.add), [b_pTt_, b_rsum], [b_rsum])
                                    yield
                                S.group("pe", [lambda i=i: nc.tensor.matmul(
                                    pAcc[i][0][:, c0:W], Vaug[:, kt, :], pts[i][0][:, c0:W],
                                    start=(kt == 0), stop=(kt == nk - 1), skip_group_check=True) for i in range(2)],
                                    [pts[0][1], pts[1][1], b_Vaug], [pAcc[0][1], pAcc[1][1]])
                                yield
                                if kt == 1 and pending:
                                    pending.pop(0)()
                            for i in range(2):
                                S.op("dve", lambda i=i: nc.vector.tensor_copy(rsbs[i][0][:, 0:W], rsums[i][0][:, 0:W]),
                                     [rsums[i][1]], [rsbs[i][1]])

                            def norm(h=h, q0=q0, nq=nq, W=W, pAcc=pAcc):
                                for i in range(2):
                                    rsb, b_rsb = rsbs[i]
                                    S.group("pe", [lambda: nc.tensor.matmul(pN[:, 0:W], onesb[:], rsb[:, 0:W],
                                                                            start=True, stop=True)],
                                            [b_onesb, b_rsb], [b_pN])
                                    S.op("act", lambda: nc.scalar.activation(out=rr[:, 0:W], in_=pN[:, 0:W], func=AF.Ln),
                                         [b_pN], [b_rr])
                                    S.op("act", lambda: nc.scalar.activation(out=rr[:, 0:W], in_=rr[:, 0:W], func=AF.Exp,
                                                                             scale=-1.0), [b_rr], [b_rr])
                                    oo, b_oo = (o1, b_o1) if i == 0 else (o2, b_o2)
                                    S.op("dve", lambda: nc.vector.tensor_tensor(oo[:, 0:W], pAcc[i][0][:, 0:W], rr[:, 0:W],
                                                                                ALU.mult), [pAcc[i][1], b_rr], [b_oo])
                                S.op("dve", lambda: nc.vector.scalar_tensor_tensor(
                                    o1[:, 0:W], o2[:, 0:W], neglam, o1[:, 0:W], ALU.mult, ALU.add),
                                    [b_o1, b_o2, b_small], [b_o1])
                                rsb, b_rsb = rsbs[0]
                                S.op("pool", lambda: nc.gpsimd.tensor_tensor(rsb[:, 0:W], o1[:, 0:W], o1[:, 0:W], ALU.mult),
                                     [b_o1], [b_rsb])
                                S.group("pe", [lambda: nc.tensor.matmul(pN[:, 0:W], onesb[:], rsb[:, 0:W],
                                                                        start=True, stop=True)], [b_onesb, b_rsb], [b_pN])
                                S.op("act", lambda: nc.scalar.activation(out=rr[:, 0:W], in_=pN[:, 0:W], func=AF.Ln,
                                                                         bias=nh384[:, 0:1], scale=1.0 / 128),
                                     [b_pN, b_nh384], [b_rr])
                                S.op("act", lambda: nc.scalar.activation(out=o2[:, 0:W], in_=rr[:, 0:W], func=AF.Exp,
                                                                         scale=-0.5), [b_rr], [b_o2])
                                S.op("dve", lambda: nc.vector.scalar_tensor_tensor(
                                    yattnT[:, h, q0 * 128: q0 * 128 + W], o1[:, 0:W], sublncol[:, 0:1], o2[:, 0:W],
                                    ALU.mult, ALU.mult), [b_o1, b_o2, b_sublncol], b_yattnT[q0:q0 + nq])

                            pending.append(norm)
                            yield

                    for _ in proj_stream(0):
                        pass
                    for h in range(8):
                        for _ in attn_stream(h):
                            pass
                        if h + 1 < 8:
                            for k_, _ in enumerate(proj_stream(h + 1)):
                                if k_ == 0 and pending:
                                    pending.pop(0)()
                        while pending:
                            pending.pop(0)()
                    S.barrier()
                if debug and s == 0:
                    out_evs.append(S.dma(dbg["d_yattnT"], yattnT[:].rearrange("p k t -> p (k t)"), reads=b_yattnT))

                if stop_after == "C":
                    raise _Stop()
                with contextlib.ExitStack() as ph:
                    ring = Ring(ph, 5, "d1")
                    tga, b_tga = alloc(ph, "tga", [128, 512])
                    tgb, b_tgb = alloc(ph, "tgb", [128, 512])
                    m1, b_m1 = alloc(ph, "m1", [128, 512])
                    m2, b_m2 = alloc(ph, "m2", [128, 512])
                    pP = [palloc(ph, "pM%d" % i, [128, 512]) for i in range(8)]
                    for j in range(2):
                        uGS, b_uGS = ring.load(U_GS[j])
                        uBRS, b_uBRS = ring.load(U_BRS[j])
                        uGA, b_uGA = ring.load(U_GA[j])
                        uBRA, b_uBRA = ring.load(U_BRA[j])
                        it = 0
                        for f4 in range(4):
                            fc = j * 4 + f4
                            for tg in range(4):
                                tsl = slice(tg * 512, (tg + 1) * 512)
                                base = (it % 2) * 4
                                it += 1
                                for k4, (unit, b_unit, src, b_src) in enumerate((
                                        (uGS, b_uGS, hT, b_hT), (uBRS, b_uBRS, yssdT, b_yssdT),
                                        (uGA, b_uGA, hT, b_hT), (uBRA, b_uBRA, yattnT, b_yattnT))):
                                    pX, b_pX = pP[base + k4]
                                    S.group("pe", [lambda kc=kc, pX=pX, unit=unit, src=src: nc.tensor.matmul(
                                        pX[:, :], unit[:, kc * 512 + f4 * 128: kc * 512 + (f4 + 1) * 128],
                                        src[:, kc, tsl], start=(kc == 0), stop=(kc == 7)) for kc in range(8)],
                                        b_src[tg * 4:(tg + 1) * 4] + [b_unit], [b_pX])
                                S.op("act", lambda base=base: nc.scalar.activation(
                                    out=tga[:], in_=pP[base][0][:, :], func=AF.Tanh, scale=0.5), [pP[base][1]], [b_tga])
                                S.op("dve", lambda base=base: nc.vector.scalar_tensor_tensor(
                                    m1[:], tga[:], 1.0, pP[base + 1][0][:, :], ALU.add, ALU.mult),
                                    [b_tga, pP[base + 1][1]], [b_m1])
                                S.op("act", lambda base=base: nc.scalar.activation(
                                    out=tgb[:], in_=pP[base + 2][0][:, :], func=AF.Tanh, scale=0.5),
                                    [pP[base + 2][1]], [b_tgb])
                                S.op("dve", lambda base=base: nc.vector.scalar_tensor_tensor(
                                    m2[:], tgb[:], 1.0, pP[base + 3][0][:, :], ALU.add, ALU.mult),
                                    [b_tgb, pP[base + 3][1]], [b_m2])
                                S.op("pool", lambda fc=fc: nc.gpsimd.tensor_tensor(mergedT[:, fc, tsl], m1[:], m2[:], ALU.add),
                                     [b_m1, b_m2], [b_mergedT])
                    S.barrier()
                if debug and s == 0:
                    out_evs.append(S.dma(dbg["d_mergedT"], mergedT[:].rearrange("p k t -> p (k t)"), reads=[b_mergedT]))

            if stop_after == "D1":
                raise _Stop()
            sq.close()

            with contextlib.ExitStack() as ph:
                ring = Ring(ph, 5, "d2")
                rowd, b_rowp = alloc(ph, "rowd", [128, 2048])
                S.dma(rowd[:], rowp_d[:, 1024:3072].partition_broadcast(128), writes=[b_rowp])
                wffn_bc = rowd[:, 0:1024]
                wfin_bc = rowd[:, 1024:2048]
                dgf, b_dgf = alloc(ph, "dgfb", [128, NFC * 384], BF16)
                S.dma(dgf[:], dsc_f, reads=[B_dsc_f], writes=[b_dgf])
                halo, b_halo = alloc(ph, "halo", [128, NFC, 2], BF16)
                S.op("pool", lambda: nc.gpsimd.memset(halo[:], 0.0), [], [b_halo])
                x1, _ = alloc(ph, "x1", [128, 4, DM])
                b_x1 = [Buf("x1_%d" % t) for t in range(4)]
                h2T, _ = alloc(ph, "h2T", [128, 8, 512], BF16)
                b_h2T = [Buf("h2T%d" % t) for t in range(4)]
                actT, _ = alloc(ph, "actT", [128, NFC, 512], BF16)
                b_actT = [Buf("actT%d" % c) for c in range(NFC)]
                gins = [alloc(ph, "gin%d" % i, [128, 514], BF16) for i in range(3)]
                xts = [alloc(ph, "xd%d" % i, [128, DM]) for i in range(4)]
                hbs = [alloc(ph, "hbd%d" % i, [128, DM], BF16) for i in range(2)]
                st2s = [alloc(ph, "std%d" % i, [128, 4]) for i in range(2)]
                thf, b_thf = alloc(ph, "thf", [128, 512])
                sg, b_sg = alloc(ph, "sg", [128, 512])
                ots = [alloc(ph, "ot%d" % i, [128, DM]) for i in range(2)]
                pQ = [palloc(ph, "pQ%d" % i, [128, 512]) for i in range(7)]
                psT, b_psT = palloc(ph, "psTd", [128, 1024], BF16)
                for tg in range(4):
                    uO = [ring.load(U_OUT[0]), ring.load(U_OUT[1])]
                    for t in range(4):
                        tok = tg * 4 + t
                        xt, b_xt = xts[t % 4]
                        S.dma(xt[:], x_d[s, tok * 128:(tok + 1) * 128, :], writes=[b_xt], q="act")
                        for half in range(2):
                            pX, b_pX = pQ[half]
                            S.group("pe", [lambda kc=kc, pX=pX, half=half: nc.tensor.matmul(
                                pX[:, :], mergedT[:, kc, tok * 128:(tok + 1) * 128],
                                uO[half][0][:, kc * 512:(kc + 1) * 512], start=(kc == 0), stop=(kc == 7))
                                for kc in range(8)], [b_mergedT, uO[half][1]], [b_pX])
                            S.op("dve", lambda pX=pX, half=half, xt=xt: nc.vector.scalar_tensor_tensor(
                                x1[:, t, half * 512:(half + 1) * 512], pX[:, :], 0.5, xt[:, half * 512:(half + 1) * 512],
                                ALU.mult, ALU.add), [b_pX, b_xt], [b_x1[t]])
                        rms_to_T(x1[:, t, :], b_x1[t], wffn_bc, b_rowp, h2T, b_h2T[t], t * 128, hbs[t % 2][0],
                                 hbs[t % 2][1], st2s[t % 2][0], st2s[t % 2][1], psT, b_psT,
                                 "act" if t % 2 == 0 else "dve")
                    d3_pending = []
                    for j in range(11):
                        uU, b_uU = ring.load(U_UP[j])
                        for sub in range(2):
                            ch = 2 * j + sub
                            pG, b_pG = pQ[ch % 2]
                            pVv, b_pVv = pQ[2 + ch % 3]
                            pCv, b_pCv = pQ[5 + ch % 2]
                            gin, b_gin = gins[ch % 3]
                            S.group("pe", [lambda kc=kc, pG=pG, sub=sub: nc.tensor.matmul(
                                pG[:, :], uU[:, kc * 512 + sub * 128: kc * 512 + (sub + 1) * 128], h2T[:, kc, :],
                                start=(kc == 0), stop=(kc == 7)) for kc in range(8)], b_h2T + [b_uU], [b_pG])
                            S.group("pe", [lambda kc=kc, pVv=pVv, sub=sub: nc.tensor.matmul(
                                pVv[:, :], uU[:, kc * 512 + 256 + sub * 128: kc * 512 + 256 + (sub + 1) * 128],
                                h2T[:, kc, :], start=(kc == 0), stop=(kc == 7)) for kc in range(8)],
                                b_h2T + [b_uU], [b_pVv])
                            S.op("pool", lambda gin=gin, ch=ch: nc.gpsimd.tensor_copy(gin[:, 0:2], halo[:, ch, :]),
                                 [b_halo], [b_gin])
                            S.op("act", lambda gin=gin, pG=pG: nc.scalar.copy(gin[:, 2:514], pG[:, :]), [b_pG], [b_gin])
                            S.op("pool", lambda gin=gin, ch=ch: nc.gpsimd.tensor_copy(halo[:, ch, :], gin[:, 512:514]),
                                 [b_gin], [b_halo])
                            def conv_tail(ch=ch, pCv=pCv, b_pCv=b_pCv, gin=gin, b_gin=b_gin, pVv=pVv, b_pVv=b_pVv):
                                fns = [lambda j3=j3: nc.tensor.matmul(
                                    pCv[:, :], dgf[:, ch * 384 + j3 * 128: ch * 384 + (j3 + 1) * 128],
                                    gin[:, j3: j3 + 512], start=(j3 == 0), stop=False) for j3 in range(3)]
                                fns.append(lambda: nc.tensor.matmul(
                                    pCv[:, :], cbf_b[0:1, ch * 128:(ch + 1) * 128], onesrow[0:1, :], start=False, stop=True))
                                S.group("pe", fns, [b_gin, b_dgf, b_cbf, b_onesrow], [b_pCv])
                                S.op("act", lambda: nc.scalar.activation(out=thf[:], in_=pCv[:, :], func=AF.Tanh),
                                     [b_pCv], [b_thf])
                                S.op("dve", lambda: nc.vector.scalar_tensor_tensor(
                                    sg[:], thf[:], 1.0, pCv[:, :], ALU.add, ALU.mult), [b_thf, b_pCv], [b_sg])
                                S.op("dve", lambda: nc.vector.tensor_tensor(
                                    actT[:, ch, :], sg[:], pVv[:, :], ALU.mult), [b_sg, b_pVv], [b_actT[ch]])

                            if d3_pending:
                                d3_pending.pop(0)()
                            d3_pending.append(conv_tail)
                    while d3_pending:
                        d3_pending.pop(0)()
                    for half in range(2):
                        for ku in range(3):
                            nk = 8 if ku < 2 else 6
                            uD, b_uD = ring.load(U_DN[half * 3 + ku], n=nk * 512)
                            for t in range(4):
                                pX, b_pX = pQ[t]
                                S.group("pe", [lambda kcl=kcl, pX=pX, t=t, ku=ku, nk=nk, uD=uD: nc.tensor.matmul(
                                    pX[:, :], actT[:, ku * 8 + kcl, t * 128:(t + 1) * 128],
                                    uD[:, kcl * 512:(kcl + 1) * 512], start=(ku == 0 and kcl == 0),
                                    stop=(ku == 2 and kcl == nk - 1), skip_group_check=True) for kcl in range(nk)],
                                    b_actT[ku * 8: ku * 8 + nk] + [b_uD], [b_pX])
                        for t in range(4):
                            pX, b_pX = pQ[t]
                            S.op("dve", lambda pX=pX, t=t, half=half: nc.vector.tensor_tensor(
                                x1[:, t, half * 512:(half + 1) * 512], pX[:, :], x1[:, t, half * 512:(half + 1) * 512],
                                ALU.add), [b_pX, b_x1[t]], [b_x1[t]])
                    for t in range(4):
                        tok = tg * 4 + t
                        st2, b_st2 = st2s[t % 2]
                        ot, b_ot = ots[t % 2]
                        S.op("act", lambda t=t, st2=st2: nc.scalar.activation(
                            out=junk[:], in_=x1[:, t, :], func=AF.Square, accum_out=st2[:, 0:1]),
                            [b_x1[t]], [b_junk, b_st2])
                        S.op("dve", lambda st2=st2: nc.vector.tensor_scalar(st2[:, 1:2], st2[:, 0:1], 1.0 / DM, NORM_EPS,
                                                                            ALU.mult, ALU.add), [b_st2], [b_st2])
                        S.op("pool", lambda st2=st2: nc.gpsimd.tensor_tensor(st2[:, 2:3], st2[:, 1:2], neghalf[:, 0:1],
                                                                             ALU.pow), [b_st2, b_small], [b_st2])
                        S.op("dve", lambda t=t, st2=st2, ot=ot: nc.vector.scalar_tensor_tensor(
                            ot[:], x1[:, t, :], st2[:, 2:3], wfin_bc, ALU.mult, ALU.mult),
                            [b_x1[t], b_st2, b_rowp], [b_ot])
                        out_evs.append(S.dma(out_d[s, tok * 128:(tok + 1) * 128, :], ot[:], reads=[b_ot], q="act"))
                S.barrier()
            so.close()
          except _Stop:
            sq.close()
            so.close()
            break

        for ev in out_evs:
            S._wait("sp", ev)
    return nc, S


def _host_layout(inp):
    f = np.float32
    W = np.asarray(inp["w_in"], f)[0]
    cols = []
    for g in range(2):
        cols.append(W[:, 1024 + g * 512: 1024 + (g + 1) * 512])
    cols.append(np.concatenate([W[:, 2048:2176], W[:, 2304:2432], W[:, 2176:2304], W[:, 2432:2560]], axis=1))
    for g in range(2):
        cols.append(W[:, g * 512:(g + 1) * 512])
    for h in range(8):
        q = W[:, 2576 + h * 128: 2576 + (h + 1) * 128]
        k = W[:, 3600 + h * 128: 3600 + (h + 1) * 128]
        v = W[:, 4624 + h * 128: 4624 + (h + 1) * 128]
        cols.append(np.concatenate([q, k, v, np.zeros_like(v)], axis=1))
    for j in range(2):
        cols.append(W[:, 4624 + j * 512: 4624 + (j + 1) * 512])
    for j in range(2):
        cols.append(W[:, 5648 + j * 512: 5648 + (j + 1) * 512])
    for j in range(2):
        cols.append(W[:, 6672 + j * 512: 6672 + (j + 1) * 512])
    for nm in ("w_branch_ssd", "w_branch_attn", "w_out"):
        M = np.asarray(inp[nm], f)[0]
        for j in range(2):
            cols.append(M[:, j * 512:(j + 1) * 512])
    U = np.asarray(inp["w_up"], f)[0]
    for j in range(11):
        cols.append(np.concatenate([U[:, (2 * j) * 128:(2 * j + 1) * 128], U[:, (2 * j + 1) * 128:(2 * j + 2) * 128],
                                    U[:, DFF + (2 * j) * 128: DFF + (2 * j + 1) * 128],
                                    U[:, DFF + (2 * j + 1) * 128: DFF + (2 * j + 2) * 128]], axis=1))
    wcat = np.ascontiguousarray(np.concatenate(cols, axis=1))
    assert wcat.shape == (DM, NU1 * 512)
    wdt = np.ascontiguousarray(W[:, 2560:2576].reshape(8, 128, 16).transpose(1, 0, 2).reshape(128, 128))
    rowp = np.concatenate([np.asarray(inp["norm_mix_w"], f)[0], np.asarray(inp["norm_ffn_w"], f)[0],
                           np.asarray(inp["final_norm_w"], f), np.asarray(inp["ssd_norm_w"], f)[0],
                           np.repeat(np.asarray(inp["ssd_d_skip"], f)[0], 64)])[None, :]
    rows = np.concatenate([np.asarray(inp["ssd_dt_bias"], f)[0], np.asarray(inp["ssd_a_log"], f)[0],
                           np.asarray(inp["subln_w"], f)[0], np.asarray(inp["lambda_q1"], f)[0],
                           np.asarray(inp["lambda_k1"], f)[0], np.asarray(inp["lambda_q2"], f)[0],
                           np.asarray(inp["lambda_k2"], f)[0]])[None, :]
    chbase = [i * 128 for i in range(8)] + [1024, 1152, 1280, 1408]
    cw = np.asarray(inp["ssd_conv_w"], f)[0]
    cb = np.asarray(inp["ssd_conv_b"], f)[0]
    convs = np.zeros((128, 48), f)
    cbs = np.zeros((1, 1536), f)
    for ch, b0 in enumerate(chbase):
        convs[:, ch * 4:(ch + 1) * 4] = cw[:, b0:b0 + 128].T
        cbs[0, ch * 128:(ch + 1) * 128] = cb[b0:b0 + 128]
    fw = np.asarray(inp["ffn_conv_w"], f)[0]
    convf = np.zeros((128, NFC * 3), f)
    for ch in range(NFC):
        convf[:, ch * 3:(ch + 1) * 3] = fw[:, ch * 128:(ch + 1) * 128].T
    cbf = np.asarray(inp["ffn_conv_b"], f)[0][None, :]
    idx = np.arange(128)
    ident = np.eye(128, dtype=f)
    negm = np.where(idx[:, None] <= idx[None, :], 0.0, -30000.0).astype(f)
    tri = (idx[:, None] <= idx[None, :]).astype(f)
    gt = (idx[:, None] > idx[None, :]).astype(f)
    src = np.array([(c // 64) * 64 + ((c % 64) + 32) % 64 for c in range(128)])
    permm = np.zeros((128, 128), f)
    permm[src, idx] = 1.0
    cmat = np.concatenate([ident, negm, tri, gt, np.ones((128, 128), f), permm], axis=1)
    onehot = np.zeros((16, 16 * 128), f)
    for hh in range(16):
        onehot[hh, hh * 128:(hh + 1) * 128] = 1.0
    inv = (1.0 / (np.float32(10000.0) ** (np.arange(0, 64, 2, dtype=f) / np.float32(64)))).astype(f)
    ang = (np.arange(SEQ, dtype=f)[:, None] * inv[None, :]).astype(f)
    cos, sin = np.cos(ang).astype(f), np.sin(ang).astype(f)
    rope = np.zeros((128, 2 * SEQ), f)
    for p in range(128):
        dd = p % 64
        rope[p, 0:SEQ] = cos[:, dd % 32]
        rope[p, SEQ:] = (-1.0 if dd < 32 else 1.0) * sin[:, dd % 32]
    return dict(wcat=wcat, wdown=np.ascontiguousarray(np.asarray(inp["w_down"], f)[0]), wdt=wdt,
                rowp=np.ascontiguousarray(rowp), rows=np.ascontiguousarray(rows), convs=convs, convf=convf,
                cbs=cbs, cbf=np.ascontiguousarray(cbf), cmat=np.ascontiguousarray(cmat), onehot=onehot.astype(ml_dtypes.bfloat16), rope=rope)


def kernel(**inputs):
    x = np.asarray(inputs["x"], np.float32)
    nb = x.shape[0] // NCORES
    shared = _host_layout(inputs)
    nc = build_program(nseq=nb)
    in_maps = []
    for c in range(NCORES):
        m = dict(shared)
        m["x"] = np.ascontiguousarray(x[c * nb:(c + 1) * nb])
        in_maps.append(m)
    res = run_bass_kernel_spmd(nc, in_maps, core_ids=list(range(NCORES)))
    return np.concatenate([np.asarray(r["out"], np.float32) for r in res.results], axis=0)
```

```python
import contextlib
import numpy as np
import ml_dtypes
import concourse.bass as bass
import concourse.mybir as mybir
from concourse.bass_utils import run_bass_kernel_spmd

F32 = mybir.dt.float32
BF16 = mybir.dt.bfloat16
AF = mybir.ActivationFunctionType
ALU = mybir.AluOpType
AX = mybir.AxisListType

NCORES = 8
SEQ = 2048
DM = 1024
NT = SEQ // 128
DFF = 2816
NFC = DFF // 128
NORM_EPS = 1e-6
SUBLN_EPS = 1e-5
LAMBDA_INIT = 0.8 - 0.6 * 1.0

U_XS = [0, 1]
U_BC = 2
U_Z = [3, 4]
U_QK = list(range(5, 13))
U_V = [13, 14]
U_GS = [15, 16]
U_GA = [17, 18]
U_BRS = [19, 20]
U_BRA = [21, 22]
U_OUT = [23, 24]
U_UP = list(range(25, 36))
NU1 = 36
U_DN = list(range(36, 42))
NUNITS = 42


class Buf:
    __slots__ = ("name", "w", "r", "excl")

    def __init__(self, name, excl=False):
        self.name = name
        self.w = None
        self.r = []
        self.excl = excl


class Sched:
    def __init__(self, nc, es, needed=None, n_dma_sems=24):
        self.nc = nc
        self.eng = {"pe": nc.tensor, "act": nc.scalar, "dve": nc.vector,
                    "pool": nc.gpsimd, "sp": nc.sync}
        self.sem, self.cnt, self.clock, self.opidx, self.semval = {}, {}, {}, {}, {}
        for e in self.eng:
            self.sem[e] = es.enter_context(nc.semaphore("s_" + e))
            self.cnt[e] = 0
            self.opidx[e] = 0
            self.clock[e] = {}
            self.semval[e] = {}
        self.dsem = [es.enter_context(nc.semaphore("s_dma%d" % i)) for i in range(n_dma_sems)]
        self.dcnt = [0] * n_dma_sems
        self.dlast = [None] * n_dma_sems
        self.dnext = 0
        self.nwaits = 0
        self.ninst = 0
        self.needed_in = needed
        self.needed = set()

    def _wait(self, e, ev):
        key, val, vc = ev
        ck = self.clock[e]
        if ck.get(key, 0) >= val:
            return
        if isinstance(key, str):
            self.needed.add((key, val))
            self.eng[e].wait_ge(self.sem[key], self.semval[key][val])
        else:
            self.eng[e].wait_ge(self.dsem[key], val)
        self.nwaits += 1
        for k, v in vc.items():
            if ck.get(k, 0) < v:
                ck[k] = v

    def _deps(self, e, reads, writes):
        for b in reads:
            if b.w is not None and not (e == "pe" and b.w[0] == "pe"):
                self._wait(e, b.w)
            if b.excl:
                for ev in b.r:
                    if ev[0] != e:
                        self._wait(e, ev)
        for b in writes:
            if b.w is not None and not (e == "pe" and b.w[0] == "pe"):
                self._wait(e, b.w)
            for ev in b.r:
                if ev[0] == e:
                    continue
                self._wait(e, ev)

    def _record(self, ev, reads, writes):
        for b in reads:
            b.r.append(ev)
        for b in writes:
            b.w = ev
            b.r = []

    def _signal(self, e, ins, reads, writes):
        self.opidx[e] += 1
        idx = self.opidx[e]
        if self.needed_in is None or (e, idx) in self.needed_in:
            self.cnt[e] += 1
            ins.then_inc(self.sem[e], 1)
            self.semval[e][idx] = self.cnt[e]
        vc = dict(self.clock[e])
        vc[e] = idx
        ev = (e, idx, vc)
        self._record(ev, reads, writes)
        return ev

    def op(self, e, fn, reads=(), writes=()):
        self._deps(e, reads, writes)
        ins = fn()
        self.ninst += 1
        return self._signal(e, ins, reads, writes)

    def group(self, e, fns, reads=(), writes=()):
        self._deps(e, reads, writes)
        ins = None
        for fn in fns:
            ins = fn()
            self.ninst += 1
        return self._signal(e, ins, reads, writes)

    def partial(self, e, fns, reads=(), writes=()):
        self._deps(e, reads, writes)
        for fn in fns:
            fn()
            self.ninst += 1

    def dma(self, out, in_, reads=(), writes=(), q="sp"):
        i = self.dnext
        self.dnext = (self.dnext + 1) % len(self.dsem)
        if self.dlast[i] is not None:
            self._wait(q, self.dlast[i])
        self._deps(q, reads, writes)
        self.dcnt[i] += 16
        self.eng[q].dma_start(out=out, in_=in_).then_inc(self.dsem[i], 16)
        self.ninst += 1
        vc = dict(self.clock[q])
        vc[i] = self.dcnt[i]
        ev = (i, self.dcnt[i], vc)
        self.dlast[i] = ev
        self._record(ev, reads, writes)
        return ev

    def barrier(self):
        evs = []
        for e in self.eng:
            if self.opidx[e] > 0:
                ck = dict(self.clock[e])
                ck[e] = self.opidx[e]
                evs.append((e, self.opidx[e], ck))
        for ev in self.dlast:
            if ev is not None:
                evs.append(ev)
        for e in self.eng:
            for ev in evs:
                if ev[0] != e:
                    self._wait(e, ev)


class _Stop(Exception):
    pass


def build_program(nseq=4, debug=False, stop_after=None):
    _, S1 = _build(nseq, debug, stop_after, None)
    nc, S2 = _build(nseq, debug, stop_after, S1.needed)
    assert S2.needed == S1.needed
    print("program: %d instructions, %d waits, %d signalling ops" % (S2.ninst, S2.nwaits, sum(S2.cnt.values())))
    return nc


def _build(nseq, debug, stop_after, needed):
    nc = bass.Bass("TRN2", target_bir_lowering=False)

    def din(name, shape, dt=F32):
        return nc.dram_tensor(name, list(shape), dt, kind="ExternalInput").ap()

    x_d = din("x", [nseq, SEQ, DM])
    wcat_d = din("wcat", [DM, NU1 * 512])
    wdn_d = din("wdown", [DFF, DM])
    wdt_d = din("wdt", [128, 8 * 16])
    rowp_d = din("rowp", [1, 5 * 1024])
    rows_d = din("rows", [1, 16 + 16 + 128 + 4 * 64])
    convs_d = din("convs", [128, 12 * 4])
    convf_d = din("convf", [128, NFC * 3])
    cbs_d = din("cbs", [1, 1536])
    cbf_d = din("cbf", [1, DFF])
    cmat_d = din("cmat", [128, 6 * 128])
    onehot_d = din("onehot", [16, 16 * 128])
    rope_d = din("rope", [128, 2 * SEQ])
    out_d = nc.dram_tensor("out", [nseq, SEQ, DM], F32, kind="ExternalOutput").ap()
    wsc = nc.dram_tensor("wsc", [NUNITS, 128, 4096], BF16, kind="Internal").ap()
    dsc_s = nc.dram_tensor("dsc_s", [128, 12 * 512], BF16, kind="Internal").ap()
    dsc_f = nc.dram_tensor("dsc_f", [128, NFC * 384], BF16, kind="Internal").ap()
    dbg = {}
    if debug:
        for nm, shp, dt in [("d_hT", [128, 8 * SEQ], BF16), ("d_yssdT", [128, 8 * SEQ], BF16),
                            ("d_yattnT", [128, 8 * SEQ], BF16), ("d_mergedT", [128, 8 * SEQ], BF16)]:
            dbg[nm] = nc.dram_tensor(nm, shp, dt, kind="ExternalOutput").ap()

    with contextlib.ExitStack() as es:
        S = Sched(nc, es, needed)
        B_wsc = [Buf("wsc%d" % u) for u in range(NUNITS)]
        B_dsc_s, B_dsc_f = Buf("dsc_s"), Buf("dsc_f")

        uid = [0]

        def alloc(stack, name, shape, dt=F32):
            uid[0] += 1
            t = stack.enter_context(nc.sbuf_tensor("sb_%s_%d" % (name, uid[0]), list(shape), dt))
            return t, Buf(name)

        def palloc(stack, name, shape, dt=F32):
            uid[0] += 1
            t = stack.enter_context(nc.psum_tensor("ps_%s_%d" % (name, uid[0]), list(shape), dt))
            return t, Buf(name, excl=True)

        cmat, b_cmat = alloc(es, "cmat", [128, 6 * 128])
        ident_f = cmat[:, 0:128]
        tri_f = cmat[:, 256:384]
        gt_f = cmat[:, 384:512]
        ones_f = cmat[:, 512:640]
        identb, b_identb = alloc(es, "identb", [128, 128], BF16)
        negmb, b_negmb = alloc(es, "negmb", [128, 128], BF16)
        big0, _ = alloc(es, "big0", [128, 8 * SEQ], BF16)
        rows, b_rows = alloc(es, "rows", [128, 416])
        dtb_bc = rows[:, 0:16]
        alog_bc = rows[:, 16:32]
        subln_bc = rows[:, 32:160]
        small, b_small = alloc(es, "small", [128, 64])
        A_bc = small[:, 0:16]
        neglam = small[:, 16:17]
        neghalf = small[:, 32:48]
        wdtb, b_wdtb = alloc(es, "wdtb", [128, 128], BF16)
        cbs_b, b_cbs = alloc(es, "cbs_b", [1, 1536], BF16)
        cbf_b, b_cbf = alloc(es, "cbf_b", [1, DFF], BF16)
        onesrow, b_onesrow = alloc(es, "onesrow", [1, 512], BF16)
        junk, b_junk = alloc(es, "junk", [128, 1024], BF16)
        onesb, b_onesb = alloc(es, "onesb", [128, 128], BF16)
        S.op("pool", lambda: nc.gpsimd.memset(onesb[:], 1.0), [], [b_onesb])

        S.dma(cmat[:], cmat_d, writes=[b_cmat])
        S.dma(rows[:], rows_d.partition_broadcast(128), writes=[b_rows])
        S.op("dve", lambda: nc.vector.tensor_copy(identb[:], cmat[:, 0:128]), [b_cmat], [b_identb])
        S.op("dve", lambda: nc.vector.tensor_copy(negmb[:], cmat[:, 128:256]), [b_cmat], [b_negmb])
        permb, b_permb = alloc(es, "permb", [128, 128], BF16)
        S.op("dve", lambda: nc.vector.tensor_copy(permb[:], cmat[:, 640:768]), [b_cmat], [b_permb])
        S.op("pool", lambda: nc.gpsimd.memset(small[:, 32:48], -0.5), [], [b_small])
        S.op("pool", lambda: nc.gpsimd.memset(onesrow[:], 1.0), [], [b_onesrow])
        S.op("dve", lambda: nc.vector.tensor_scalar(subln_bc, subln_bc, 1.0 - LAMBDA_INIT, None, ALU.mult),
             [b_rows], [b_rows])
        S.op("act", lambda: nc.scalar.activation(out=A_bc, in_=alog_bc, func=AF.Exp), [b_rows], [b_small])
        S.op("dve", lambda: nc.vector.tensor_scalar(A_bc, A_bc, -1.0, None, ALU.mult), [b_small], [b_small])
        with contextlib.ExitStack() as ph:
            tl, b_tl = alloc(ph, "tl", [128, 128])
            sl, b_sl = alloc(ph, "sl", [128, 4])
            S.op("dve", lambda: nc.vector.tensor_tensor(tl[:, 0:64], rows[:, 160:224], rows[:, 224:288], ALU.mult),
                 [b_rows], [b_tl])
            S.op("dve", lambda: nc.vector.tensor_tensor(tl[:, 64:128], rows[:, 288:352], rows[:, 352:416], ALU.mult),
                 [b_rows], [b_tl])
            S.op("dve", lambda: nc.vector.reduce_sum(sl[:, 0:2], tl[:].rearrange("p (a b) -> p a b", a=2), AX.X),
                 [b_tl], [b_sl])
            S.op("act", lambda: nc.scalar.activation(out=sl[:, 2:4], in_=sl[:, 0:2], func=AF.Exp), [b_sl], [b_sl])
            S.op("dve", lambda: nc.vector.tensor_tensor(sl[:, 0:1], sl[:, 3:4], sl[:, 2:3], ALU.subtract),
                 [b_sl], [b_sl])
            S.op("dve", lambda: nc.vector.tensor_scalar(neglam, sl[:, 0:1], -LAMBDA_INIT, None, ALU.add),
                 [b_sl], [b_small])
            stg, b_stg = alloc(ph, "stg0", [128, DFF])
            S.dma(stg[:, 0:128], wdt_d, writes=[b_stg])
            S.op("dve", lambda: nc.vector.tensor_copy(wdtb[:], stg[:, 0:128]), [b_stg], [b_wdtb])
            S.dma(stg[0:1, 0:1536], cbs_d, writes=[b_stg])
            S.op("dve", lambda: nc.vector.tensor_scalar(cbs_b[:], stg[0:1, 0:1536], 0.5, None, ALU.mult),
                 [b_stg], [b_cbs])
            S.dma(stg[0:1, 0:DFF], cbf_d, writes=[b_stg])
            S.op("dve", lambda: nc.vector.tensor_scalar(cbf_b[:], stg[0:1, 0:DFF], 0.5, None, ALU.mult),
                 [b_stg], [b_cbf])
            cw, b_cw = alloc(ph, "cw", [128, 48 + NFC * 3])
            S.dma(cw[:, 0:48], convs_d, writes=[b_cw])
            S.dma(cw[:, 48:48 + NFC * 3], convf_d, writes=[b_cw])
            dgs, b_dgs = alloc(ph, "dgs", [128, 12 * 512], BF16)
            dgf, b_dgf = alloc(ph, "dgf", [128, NFC * 384], BF16)
            for ch in range(12):
                for j in range(4):
                    eng = "dve" if (ch + j) % 2 == 0 else "pool"
                    e_ = nc.vector if eng == "dve" else nc.gpsimd
                    S.op(eng, lambda e_=e_, ch=ch, j=j: e_.tensor_scalar(
                        dgs[:, ch * 512 + j * 128: ch * 512 + (j + 1) * 128], ident_f,
                        cw[:, ch * 4 + j: ch * 4 + j + 1], 0.5, ALU.mult, ALU.mult), [b_cmat, b_cw], [b_dgs])
            for ch in range(NFC):
                for j in range(3):
                    eng = "dve" if (ch + j) % 2 == 0 else "pool"
                    e_ = nc.vector if eng == "dve" else nc.gpsimd
                    S.op(eng, lambda e_=e_, ch=ch, j=j: e_.tensor_scalar(
                        dgf[:, ch * 384 + j * 128: ch * 384 + (j + 1) * 128], ident_f,
                        cw[:, 48 + ch * 3 + j: 48 + ch * 3 + j + 1], 0.5, ALU.mult, ALU.mult),
                        [b_cmat, b_cw], [b_dgf])
            S.dma(dsc_s, dgs[:], reads=[b_dgs], writes=[B_dsc_s])
            S.dma(dsc_f, dgf[:], reads=[b_dgf], writes=[B_dsc_f])
            S.barrier()

        with contextlib.ExitStack() as ph:
            stgs = [alloc(ph, "wst%d" % i, [128, 4096]) for i in range(3)]
            cvts = [alloc(ph, "wcv%d" % i, [128, 4096], BF16) for i in range(3)]
            for u in range(NUNITS if stop_after != "const" else 0):
                st, b_st = stgs[u % 3]
                cv, b_cv = cvts[u % 3]
                if u < NU1:
                    src = wcat_d[:, u * 512:(u + 1) * 512].rearrange("(k p) c -> p k c", p=128)
                    S.dma(st[:].rearrange("p (k c) -> p k c", k=8), src, writes=[b_st])
                    nk = 8
                else:
                    half, ku = divmod(u - NU1, 3)
                    nk = 8 if ku < 2 else 6
                    src = wdn_d[ku * 1024: ku * 1024 + nk * 128, half * 512:(half + 1) * 512].rearrange(
                        "(k p) c -> p k c", p=128)
                    S.dma(st[:, 0:nk * 512].rearrange("p (k c) -> p k c", k=nk), src, writes=[b_st])
                n = nk * 512
                h1 = n // 2
                S.op("dve", lambda cv=cv, st=st, h1=h1: nc.vector.tensor_copy(cv[:, 0:h1], st[:, 0:h1]),
                     [b_st], [b_cv])
                S.op("act", lambda cv=cv, st=st, h1=h1, n=n: nc.scalar.copy(cv[:, h1:n], st[:, h1:n]),
                     [b_st], [b_cv])
                S.dma(wsc[u, :, 0:n], cv[:, 0:n], reads=[b_cv], writes=[B_wsc[u]], q="pool")
            S.barrier()

        class Ring:
            def __init__(self, stack, n, tag, view=None):
                if view is None:
                    self.slots = [alloc(stack, "ring%s%d" % (tag, i), [128, 4096], BF16) for i in range(n)]
                    self.slots = [(t[:], b) for t, b in self.slots]
                else:
                    self.slots = [(view[:, i * 4096:(i + 1) * 4096], Buf("ringv%d" % i)) for i in range(n)]
                self.i = 0

            def load(self, u, n=4096):
                t, b = self.slots[self.i]
                self.i = (self.i + 1) % len(self.slots)
                S.dma(t[:, 0:n], wsc[u, :, 0:n], reads=[B_wsc[u]], writes=[b])
                return t, b

        def rms_to_T(src, b_src, w_bc, b_w, dstT, b_dst, col0, hb, b_hb, st2, b_st2, psT, b_psT, evac_eng):
            import os
            lvl = int(os.environ.get("K_DBG_A", "9"))
            if lvl < 2:
                return
            S.op("act", lambda: nc.scalar.activation(out=junk[:], in_=src, func=AF.Square, accum_out=st2[:, 0:1]),
                 [b_src], [b_junk, b_st2])
            if lvl < 3:
                return
            S.op("dve", lambda: nc.vector.tensor_scalar(st2[:, 1:2], st2[:, 0:1], 1.0 / DM, NORM_EPS,
                                                        ALU.mult, ALU.add), [b_st2], [b_st2])
            S.op("pool", lambda: nc.gpsimd.tensor_tensor(st2[:, 2:3], st2[:, 1:2], neghalf[:, 0:1], ALU.pow),
                 [b_st2, b_small], [b_st2])
            if lvl < 4:
                return
            S.op("dve", lambda: nc.vector.scalar_tensor_tensor(hb[:], src, st2[:, 2:3], w_bc, ALU.mult, ALU.mult),
                 [b_src, b_st2, b_w], [b_hb])
            if lvl < 5:
                return
            S.group("pe", [lambda kc=kc: nc.tensor.transpose(psT[:, kc * 128:(kc + 1) * 128],
                                                            hb[:, kc * 128:(kc + 1) * 128], identb[:])
                           for kc in range(8)], [b_hb, b_identb], [b_psT])
            if lvl < 6:
                return
            o_ap = dstT[:, :, col0:col0 + 128]
            i_ap = psT[:, :].rearrange("p (k j) -> p k j", k=8)
            if evac_eng == "act":
                S.op("act", lambda: nc.scalar.copy(o_ap, i_ap), [b_psT], [b_dst])
            else:
                S.op("dve", lambda: nc.vector.tensor_copy(o_ap, i_ap), [b_psT], [b_dst])

        out_evs = []

        for s in range(nseq if stop_after not in ("const", "pre") else 0):
          try:
            so = contextlib.ExitStack()
            mergedT = big0[:].rearrange("p (k t) -> p k t", k=8)
            b_mergedT = Buf("mergedT")
            sq = contextlib.ExitStack()
            if True:
                hT, _ = alloc(sq, "hT", [128, 8, SEQ], BF16)
                b_hT = [Buf("hT%d" % t) for t in range(NT)]
                yssdT, _ = alloc(sq, "yssdT", [128, 8, SEQ], BF16)
                b_yssdT = [Buf("yssdT%d" % t) for t in range(NT)]

                with contextlib.ExitStack() as ph:
                    xts = [alloc(ph, "xa%d" % i, [128, DM]) for i in range(4)]
                    hbs = [alloc(ph, "hba%d" % i, [128, DM], BF16) for i in range(4)]
                    st2s = [alloc(ph, "sta%d" % i, [128, 4]) for i in range(4)]
                    psTs = [palloc(ph, "psTa%d" % i, [128, 1024], BF16) for i in range(4)]
                    wmix, b_rowp = alloc(ph, "wmix", [128, DM])
                    wmix_bc = wmix[:]
                    S.dma(wmix[:], rowp_d[:, 0:1024].partition_broadcast(128), writes=[b_rowp])

                    def a_stats(t):
                        xt, b_xt = xts[t % 4]
                        st2, b_st2 = st2s[t % 4]
                        S.dma(xt[:], x_d[s, t * 128:(t + 1) * 128, :], writes=[b_xt])
                        S.op("act", lambda: nc.scalar.activation(out=junk[:], in_=xt[:], func=AF.Square,
                                                                 accum_out=st2[:, 0:1]), [b_xt], [b_junk, b_st2])
                        S.op("dve", lambda: nc.vector.tensor_scalar(st2[:, 1:2], st2[:, 0:1], 1.0 / DM, NORM_EPS,
                                                                    ALU.mult, ALU.add), [b_st2], [b_st2])
                        S.op("pool", lambda: nc.gpsimd.tensor_tensor(st2[:, 2:3], st2[:, 1:2], neghalf[:, 0:1],
                                                                     ALU.pow), [b_st2, b_small], [b_st2])

                    def a_scale(t):
                        xt, b_xt = xts[t % 4]
                        st2, b_st2 = st2s[t % 4]
                        hb, b_hb = hbs[t % 4]
                        psT, b_psT = psTs[t % 4]
                        S.op("dve", lambda: nc.vector.scalar_tensor_tensor(hb[:], xt[:], st2[:, 2:3], wmix_bc,
                                                                           ALU.mult, ALU.mult),
                             [b_xt, b_st2, b_rowp], [b_hb])
                        S.group("pe", [lambda kc=kc: nc.tensor.transpose(psT[:, kc * 128:(kc + 1) * 128],
                                                                        hb[:, kc * 128:(kc + 1) * 128], identb[:])
                                       for kc in range(8)], [b_hb, b_identb], [b_psT])

                    def a_evac(t):
                        psT, b_psT = psTs[t % 4]
                        o_ap = hT[:, :, t * 128:(t + 1) * 128]
                        i_ap = psT[:, :].rearrange("p (k j) -> p k j", k=8)
                        if t % 2 == 0:
                            S.op("act", lambda: nc.scalar.copy(o_ap, i_ap), [b_psT], [b_hT[t]])
                        else:
                            S.op("dve", lambda: nc.vector.tensor_copy(o_ap, i_ap), [b_psT], [b_hT[t]])

                    for i in range(NT + 2):
                        if i < NT:
                            a_stats(i)
                        if 0 <= i - 1 < NT:
                            a_scale(i - 1)
                        if 0 <= i - 2 < NT:
                            a_evac(i - 2)
                    S.barrier()
                if debug and s == 0:
                    out_evs.append(S.dma(dbg["d_hT"], hT[:].rearrange("p k t -> p (k t)"), reads=b_hT))

                if stop_after == "A":
                    raise _Stop()
                with contextlib.ExitStack() as ph:
                    ring = Ring(ph, 4, "b", view=big0)
                    rowb, b_rowp = alloc(ph, "rowb", [128, 2048])
                    S.dma(rowb[:], rowp_d[:, 3072:5120].partition_broadcast(128), writes=[b_rowp])
                    ssdnw_bc = rowb[:, 0:1024]
                    drep_bc = rowb[:, 1024:2048]
                    onehot, b_onehot = alloc(ph, "onehot", [16, 16 * 128])
                    S.dma(onehot[:], onehot_d, writes=[b_onehot])
                    dgs, b_dgs = alloc(ph, "dgsb", [128, 12 * 512], BF16)
                    S.dma(dgs[:], dsc_s, reads=[B_dsc_s], writes=[b_dgs])
                    cin, b_cin = alloc(ph, "cin", [128, 3 + SEQ], BF16)
                    cout, b_cout = alloc(ph, "cout", [128, SEQ], BF16)
                    BT, b_BT = alloc(ph, "BT", [128, SEQ], BF16)
                    CT, b_CT = alloc(ph, "CT", [128, SEQ], BF16)
                    xs_tok, b_xs_tok = alloc(ph, "xs_tok", [128, NT, 512], BF16)
                    Btok, b_Btok = alloc(ph, "Btok", [128, NT, 128], BF16)
                    dtt, b_dtt = alloc(ph, "dtt", [128, NT, 16])
                    dA, b_dA = alloc(ph, "dA", [128, NT, 16])
                    ex, b_ex = alloc(ph, "ex", [128, NT, 48])
                    nacs, b_nacs = alloc(ph, "nacs", [128, NT, 16])
                    acsTs = [alloc(ph, "acsT%d" % i, [16, 128]) for i in range(2)]
                    cbT, b_cbT = alloc(ph, "cbT", [128, 128])
                    Es = [alloc(ph, "E%d" % i, [128, 128]) for i in range(2)]
                    MTs = [alloc(ph, "MT%d" % i, [128, 128], BF16) for i in range(16)]
                    Xcs = [alloc(ph, "Xc%d" % i, [128, 512], BF16) for i in range(2)]
                    Xds = [alloc(ph, "Xd%d" % i, [128, 512], BF16) for i in range(2)]
                    Sf, b_Sf = alloc(ph, "Sf", [128, 512])
                    Sb, b_Sb = alloc(ph, "Sb", [128, 512], BF16)
                    y1, b_y1 = alloc(ph, "y1", [128, 512])
                    y2, b_y2 = alloc(ph, "y2", [128, 512])
                    sk, b_sk = y1, b_y1
                    zz, b_zz = alloc(ph, "zz", [128, 512])
                    ygns = [alloc(ph, "ygn%d" % i, [128, 512], BF16) for i in range(2)]
                    th, b_th = alloc(ph, "th", [128, 512])
                    tz, b_tz = th, b_th
                    tmpa, b_tmpa = th[:, 0:256], b_th
                    tmpb, b_tmpb = y1[:, 0:256], b_y1
                    tmpc, b_tmpc = y2[:, 0:256], b_y2
                    stb, b_stb = alloc(ph, "stb", [128, 4])
                    pA, b_pA = palloc(ph, "pA", [128, 512])
                    pB, b_pB = palloc(ph, "pB", [128, 512])
                    pC, b_pC = palloc(ph, "pC", [128, 512])
                    pD, b_pD = palloc(ph, "pD", [128, 512])
                    pE, b_pE = palloc(ph, "pE", [128, 512])
                    pF, b_pF = palloc(ph, "pF", [128, 512])
                    pT, b_pT = palloc(ph, "pTb", [128, 1024], BF16)
                    pT2, b_pT2 = palloc(ph, "pTb2", [128, 1024], BF16)

                    S.op("pool", lambda: nc.gpsimd.memset(cin[:, 0:3], 0.0), [], [b_cin])

                    S.group("pe", [lambda c=c, kc=kc: nc.tensor.matmul(
                        pA[:, c * 16:(c + 1) * 16], hT[:, kc, c * 128:(c + 1) * 128], wdtb[:, kc * 16:(kc + 1) * 16],
                        start=(kc == 0), stop=(kc == 7), skip_group_check=True)
                        for c in range(NT) for kc in range(8)], b_hT + [b_wdtb], [b_pA])
                    S.op("dve", lambda: nc.vector.tensor_tensor(
                        tmpa.rearrange("p (c h) -> p c h", h=16), pA[:, 0:256].rearrange("p (c h) -> p c h", h=16),
                        dtb_bc.unsqueeze(1).to_broadcast([128, NT, 16]), ALU.add), [b_pA, b_rows], [b_tmpa])
                    S.op("act", lambda: nc.scalar.activation(out=tmpb, in_=tmpa, func=AF.Abs), [b_tmpa], [b_tmpb])
                    S.op("act", lambda: nc.scalar.activation(out=tmpb, in_=tmpb, func=AF.Exp, scale=-1.0),
                         [b_tmpb], [b_tmpb])
                    S.op("act", lambda: nc.scalar.activation(out=tmpc, in_=tmpb, func=AF.Ln, bias=1.0),
                         [b_tmpb], [b_tmpc])
                    S.op("dve", lambda: nc.vector.scalar_tensor_tensor(
                        dtt[:].rearrange("p c h -> p (c h)"), tmpa, 0.0, tmpc, ALU.max, ALU.add),
                        [b_tmpa, b_tmpc], [b_dtt])
                    S.op("dve", lambda: nc.vector.tensor_tensor(
                        dA[:], dtt[:], A_bc.unsqueeze(1).to_broadcast([128, NT, 16]), ALU.mult),
                        [b_dtt, b_small], [b_dA])
                    for half in range(2):
                        pX, b_pX = (pB, b_pB) if half == 0 else (pC, b_pC)
                        fns = []
                        for cc in range(8):
                            c = half * 8 + cc
                            for k3, lh in enumerate((tri_f, gt_f, ones_f)):
                                fns.append(lambda c=c, cc=cc, k3=k3, lh=lh: nc.tensor.matmul(
                                    pX[:, cc * 48 + k3 * 16: cc * 48 + (k3 + 1) * 16], lh, dA[:, c, :],
                                    start=True, stop=True, skip_group_check=True))
                        S.group("pe", fns, [b_dA, b_cmat], [b_pX])
                        S.op("act", lambda pX=pX, half=half: nc.scalar.activation(
                            out=ex[:, half * 8:(half + 1) * 8, :],
                            in_=pX[:, 0:384].rearrange("p (c k) -> p c k", k=48), func=AF.Exp), [b_pX], [b_ex])
                        S.op("dve", lambda pX=pX, half=half: nc.vector.tensor_scalar(
                            nacs[:, half * 8:(half + 1) * 8, :],
                            pX[:, 0:384].rearrange("p (c k) -> p c k", k=48)[:, :, 0:16], -1.0, None, ALU.mult),
                            [b_pX], [b_nacs])
                    if stop_after == "B0":
                        S.barrier()
                        ph.close()
                        raise _Stop()

                    for g in range(2):
                        uXS, b_uXS = ring.load(U_XS[g])
                        uBC, b_uBC = ring.load(U_BC)
                        uZ, b_uZ = ring.load(U_Z[g])
                        chans = [("xs", i) for i in range(4)] + [("B", 0), ("C", 0)]
                        for kind, i in chans:
                            if kind == "xs":
                                unit, b_unit, cb0, ch = uXS, b_uXS, i * 128, g * 4 + i
                            elif kind == "B":
                                unit, b_unit, cb0, ch = uBC, b_uBC, g * 256, 8 + g
                            else:
                                unit, b_unit, cb0, ch = uBC, b_uBC, g * 256 + 128, 10 + g
                            for tg in range(4):
                                pX, b_pX = (pA, b_pA) if tg % 2 == 0 else (pB, b_pB)
                                S.group("pe", [lambda kc=kc, pX=pX, unit=unit, cb0=cb0, tg=tg: nc.tensor.matmul(
                                    pX[:, :], unit[:, kc * 512 + cb0: kc * 512 + cb0 + 128],
                                    hT[:, kc, tg * 512:(tg + 1) * 512], start=(kc == 0), stop=(kc == 7))
                                    for kc in range(8)], b_hT[tg * 4:(tg + 1) * 4] + [b_unit], [b_pX])
                                S.op("act", lambda pX=pX, tg=tg: nc.scalar.copy(
                                    cin[:, 3 + tg * 512: 3 + (tg + 1) * 512], pX[:, :]), [b_pX], [b_cin])
                            for tg in range(4):
                                pX, b_pX = (pC, b_pC) if tg % 2 == 0 else (pD, b_pD)
                                fns = [lambda j=j, pX=pX, ch=ch, tg=tg: nc.tensor.matmul(
                                    pX[:, :], dgs[:, ch * 512 + j * 128: ch * 512 + (j + 1) * 128],
                                    cin[:, tg * 512 + j: tg * 512 + j + 512], start=(j == 0), stop=False)
                                    for j in range(4)]
                                fns.append(lambda pX=pX, ch=ch: nc.tensor.matmul(
                                    pX[:, :], cbs_b[0:1, ch * 128:(ch + 1) * 128], onesrow[0:1, :],
                                    start=False, stop=True))
                                S.group("pe", fns, [b_cin, b_dgs, b_cbs, b_onesrow], [b_pX])
                                S.op("act", lambda pX=pX: nc.scalar.activation(out=th[:], in_=pX[:, :], func=AF.Tanh),
                                     [b_pX], [b_th])
                                dst, b_dst = (cout, b_cout) if kind == "xs" else ((BT, b_BT) if kind == "B" else (CT, b_CT))
                                S.op("dve", lambda pX=pX, dst=dst, tg=tg: nc.vector.scalar_tensor_tensor(
                                    dst[:, tg * 512:(tg + 1) * 512], th[:], 1.0, pX[:, :], ALU.add, ALU.mult),
                                    [b_th, b_pX], [b_dst])
                            if kind in ("xs", "B"):
                                src, b_src = (cout, b_cout) if kind == "xs" else (BT, b_BT)
                                for hh in range(2):
                                    pX, b_pX = (pT, b_pT) if hh == 0 else (pT2, b_pT2)
                                    S.group("pe", [lambda t=t, pX=pX, src=src, hh=hh: nc.tensor.transpose(
                                        pX[:, t * 128:(t + 1) * 128], src[:, (hh * 8 + t) * 128:(hh * 8 + t + 1) * 128],
                                        identb[:]) for t in range(8)], [b_src, b_identb], [b_pX])
                                    if kind == "xs":
                                        o_ap = xs_tok[:, hh * 8:(hh + 1) * 8, i * 128:(i + 1) * 128]
                                        b_o = b_xs_tok
                                    else:
                                        o_ap = Btok[:, hh * 8:(hh + 1) * 8, :]
                                        b_o = b_Btok
                                    i_ap = pX[:, :].rearrange("p (t j) -> p t j", t=8)
                                    if hh == 0:
                                        S.op("act", lambda o_ap=o_ap, i_ap=i_ap: nc.scalar.copy(o_ap, i_ap), [b_pX], [b_o])
                                    else:
                                        S.op("dve", lambda o_ap=o_ap, i_ap=i_ap: nc.vector.tensor_copy(o_ap, i_ap),
                                             [b_pX], [b_o])

                        if stop_after == "B1":
                            S.barrier()
                            ph.close()
                            raise _Stop()
                        def front_pre(c):
                            csl = slice(c * 128, (c + 1) * 128)
                            acsT, b_acsT = acsTs[c % 2]
                            S.group("pe", [lambda: nc.tensor.matmul(pA[0:16, 0:128], dA[:, c, :], tri_f,
                                                                    start=True, stop=True)],
                                    [b_dA, b_cmat], [b_pA])
                            S.op("dve", lambda: nc.vector.tensor_copy(acsT[:], pA[0:16, 0:128]), [b_pA], [b_acsT])
                            S.group("pe", [lambda: nc.tensor.matmul(pB[:, 0:128], BT[:, csl], CT[:, csl],
                                                                    start=True, stop=True)], [b_BT, b_CT], [b_pB])
                            S.op("dve", lambda: nc.vector.tensor_copy(cbT[:], pB[:, 0:128]), [b_pB], [b_cbT])
                            Xc, b_Xc = Xcs[c % 2]
                            Xd, b_Xd = Xds[c % 2]
                            S.op("dve", lambda: nc.vector.tensor_tensor(
                                Xc[:].rearrange("p (h d) -> p h d", h=8),
                                xs_tok[:, c, :].rearrange("p (h d) -> p h d", h=8),
                                dtt[:, c, g * 8:(g + 1) * 8].unsqueeze(2).to_broadcast([128, 8, 64]), ALU.mult),
                                [b_xs_tok, b_dtt], [b_Xc])
                            S.op("pool", lambda: nc.gpsimd.tensor_tensor(
                                Xd[:].rearrange("p (h d) -> p h d", h=8), Xc[:].rearrange("p (h d) -> p h d", h=8),
                                ex[:, c, 16 + g * 8: 16 + (g + 1) * 8].unsqueeze(2).to_broadcast([128, 8, 64]),
                                ALU.mult), [b_Xc, b_ex], [b_Xd])

                        def front_head(c, h):
                            hh = g * 8 + h
                            acsT, b_acsT = acsTs[c % 2]
                            pX, b_pX = (pC, b_pC) if h % 2 == 0 else (pD, b_pD)
                            E, b_E = Es[h % 2]
                            MT, b_MT = MTs[(c % 2) * 8 + h]
                            S.group("pe", [
                                lambda: nc.tensor.matmul(pX[:, 0:128], onehot[0:16, hh * 128:(hh + 1) * 128], acsT[:],
                                                         start=True, stop=False),
                                lambda: nc.tensor.matmul(pX[:, 0:128], identb[:], negmb[:], start=False, stop=True)],
                                [b_onehot, b_acsT, b_identb, b_negmb], [b_pX])
                            S.op("act", lambda: nc.scalar.activation(
                                out=E[:], in_=pX[:, 0:128], func=AF.Exp, bias=nacs[:, c, hh:hh + 1]),
                                [b_pX, b_nacs], [b_E])
                            S.op("pool", lambda: nc.gpsimd.tensor_tensor(MT[:], E[:], cbT[:], ALU.mult),
                                 [b_E, b_cbT], [b_MT])

                        def mid(c):
                            csl = slice(c * 128, (c + 1) * 128)
                            Xc, b_Xc = Xcs[c % 2]
                            Xd, b_Xd = Xds[c % 2]
                            S.group("pe", [lambda h=h: nc.tensor.matmul(
                                pE[:, h * 64:(h + 1) * 64], MTs[(c % 2) * 8 + h][0][:], Xc[:, h * 64:(h + 1) * 64],
                                start=(h == 0), stop=(h == 7), skip_group_check=True) for h in range(8)],
                                [MTs[(c % 2) * 8 + h][1] for h in range(8)] + [b_Xc], [b_pE])
                            if c > 0:
                                S.group("pe", [lambda: nc.tensor.matmul(pF[:, :], CT[:, csl], Sb[:],
                                                                        start=True, stop=True)],
                                        [b_CT, b_Sb], [b_pF])
                                S.op("dve", lambda: nc.vector.tensor_tensor(
                                    y1[:].rearrange("p (h d) -> p h d", h=8),
                                    pF[:, :].rearrange("p (h d) -> p h d", h=8),
                                    ex[:, c, g * 8:(g + 1) * 8].unsqueeze(2).to_broadcast([128, 8, 64]), ALU.mult),
                                    [b_pF, b_ex], [b_y1])
                                S.op("dve", lambda: nc.vector.tensor_tensor(y2[:], pE[:, :], y1[:], ALU.add),
                                     [b_pE, b_y1], [b_y2])
                            else:
                                S.op("dve", lambda: nc.vector.tensor_copy(y2[:], pE[:, :]), [b_pE], [b_y2])
                            if c < NT - 1:
                                S.group("pe", [lambda: nc.tensor.matmul(pF[:, :], Btok[:, c, :], Xd[:],
                                                                        start=True, stop=True)],
                                        [b_Btok, b_Xd], [b_pF])
                                if c == 0:
                                    S.op("dve", lambda: nc.vector.tensor_copy(Sf[:], pF[:, :]), [b_pF], [b_Sf])
                                else:
                                    S.op("dve", lambda: nc.vector.tensor_tensor(
                                        Sf[:].rearrange("p (h d) -> p h d", h=8), Sf[:].rearrange("p (h d) -> p h d", h=8),
                                        ex[:, c, 32 + g * 8: 32 + (g + 1) * 8].unsqueeze(2).to_broadcast([128, 8, 64]),
                                        ALU.mult), [b_Sf, b_ex], [b_Sf])
                                    S.op("dve", lambda: nc.vector.tensor_tensor(Sf[:], Sf[:], pF[:, :], ALU.add),
                                         [b_Sf, b_pF], [b_Sf])
                                S.op("act", lambda: nc.scalar.copy(Sb[:], Sf[:]), [b_Sf], [b_Sb])

                        def tail_ops(c):
                            csl = slice(c * 128, (c + 1) * 128)
                            ygn, b_ygn = ygns[c % 2]
                            ops = []
                            ops.append(lambda: S.op("pool", lambda: nc.gpsimd.tensor_tensor(
                                sk[:], xs_tok[:, c, :], drep_bc[:, g * 512:(g + 1) * 512], ALU.mult),
                                [b_xs_tok, b_rowp], [b_sk]))
                            ops.append(lambda: S.group("pe", [lambda kc=kc: nc.tensor.matmul(
                                pA[:, :], hT[:, kc, csl], uZ[:, kc * 512:(kc + 1) * 512],
                                start=(kc == 0), stop=(kc == 7)) for kc in range(8)], [b_hT[c], b_uZ], [b_pA]))
                            ops.append(lambda: S.op("dve", lambda: nc.vector.tensor_tensor(y2[:], y2[:], sk[:], ALU.add),
                                                    [b_y2, b_sk], [b_y2]))
                            ops.append(lambda: S.op("act", lambda: nc.scalar.activation(
                                out=tz[:], in_=pA[:, :], func=AF.Tanh, scale=0.5), [b_pA], [b_tz]))
                            ops.append(lambda: S.op("dve", lambda: nc.vector.scalar_tensor_tensor(
                                zz[:], tz[:], 1.0, pA[:, :], ALU.add, ALU.mult), [b_tz, b_pA], [b_zz]))
                            ops.append(lambda: S.op("dve", lambda: nc.vector.tensor_tensor(y2[:], y2[:], zz[:], ALU.mult),
                                                    [b_y2, b_zz], [b_y2]))
                            ops.append(lambda: S.op("act", lambda: nc.scalar.activation(
                                out=junk[:, 0:512], in_=y2[:], func=AF.Square, accum_out=stb[:, 0:1]),
                                [b_y2], [b_junk, b_stb]))

                            def stats():
                                S.op("dve", lambda: nc.vector.tensor_scalar(stb[:, 1:2], stb[:, 0:1], 1.0 / 512,
                                                                            4.0 * NORM_EPS, ALU.mult, ALU.add),
                                     [b_stb], [b_stb])
                                S.op("pool", lambda: nc.gpsimd.tensor_tensor(stb[:, 2:3], stb[:, 1:2], neghalf[:, 0:1],
                                                                             ALU.pow), [b_stb, b_small], [b_stb])
                                S.op("dve", lambda: nc.vector.scalar_tensor_tensor(
                                    ygn[:], y2[:], stb[:, 2:3], ssdnw_bc[:, g * 512:(g + 1) * 512], ALU.mult, ALU.mult),
                                    [b_y2, b_stb, b_rowp], [b_ygn])
                            ops.append(stats)

                            def fin():
                                S.group("pe", [lambda j=j: nc.tensor.transpose(
                                    pT[:, j * 128:(j + 1) * 128], ygn[:, j * 128:(j + 1) * 128], identb[:])
                                    for j in range(4)], [b_ygn, b_identb], [b_pT])
                                S.op("act", lambda: nc.scalar.copy(
                                    yssdT[:, g * 4:(g + 1) * 4, csl], pT[:, 0:512].rearrange("p (k j) -> p k j", k=4)),
                                    [b_pT], [b_yssdT[c]])
                            return ops, fin

                        front_pre(0)
                        for h in range(8):
                            front_head(0, h)
                        for c in range(NT):
                            mid(c)
                            ops, fin = tail_ops(c)
                            if c + 1 < NT:
                                front_pre(c + 1)
                                for h in range(8):
                                    front_head(c + 1, h)
                                    if h < len(ops):
                                        ops[h]()
                                for o_ in ops[8:]:
                                    o_()
                            else:
                                for o_ in ops:
                                    o_()
                            fin()
                    S.barrier()
                if debug and s == 0:
                    out_evs.append(S.dma(dbg["d_yssdT"], yssdT[:].rearrange("p k t -> p (k t)"), reads=b_yssdT))

                if stop_after == "B":
                    raise _Stop()
                yattnT, _ = alloc(sq, "yattnT", [128, 8, SEQ], BF16)
                b_yattnT = [Buf("yattnT%d" % t) for t in range(NT)]

                with contextlib.ExitStack() as ph:
                    ring = Ring(ph, 4, "c", view=big0)
                    rope, b_rope = alloc(ph, "rope", [128, 2 * SEQ])
                    S.dma(rope[:], rope_d, writes=[b_rope])
                    QTs = [alloc(ph, "QT%d" % i, [128, SEQ], BF16) for i in range(1)] * 2
                    KTs = [alloc(ph, "KT%d" % i, [128, SEQ], BF16) for i in range(1)] * 2
                    Vaugs = [alloc(ph, "Vaug%d" % i, [128, NT, 128], BF16) for i in range(1)] * 2
                    t1s = [alloc(ph, "t1_%d" % i, [128, 512]) for i in range(2)]
                    t2s = [alloc(ph, "t2_%d" % i, [128, 512]) for i in range(2)]
                    pTs_ = [[alloc(ph, "pT%d%d" % (i, j), [128, 512], BF16) for j in range(2)] for i in range(2)]
                    qbs = [alloc(ph, "qb%d" % i, [128, 512], BF16) for i in range(2)]
                    o1, b_o1 = alloc(ph, "o1", [128, 512])
                    o2, b_o2 = alloc(ph, "o2", [128, 512])
                    rr, b_rr = alloc(ph, "rr", [128, 512])
                    rsums = [alloc(ph, "rsum%d" % i, [128, 512]) for i in range(2)]
                    rsbs = [alloc(ph, "rsb%d" % i, [128, 512], BF16) for i in range(2)]
                    nh384, b_nh384 = alloc(ph, "epsc", [128, 2])
                    S.op("pool", lambda: nc.gpsimd.memset(nh384[:], SUBLN_EPS), [], [b_nh384])
                    sublncol, b_sublncol = alloc(ph, "sublncol", [128, 2])
                    S.dma(sublncol[:, 0:1], rows_d[0:1, 32:160].rearrange("o e -> e o"), writes=[b_sublncol])
                    S.op("dve", lambda: nc.vector.tensor_scalar(sublncol[:, 0:1], sublncol[:, 0:1], 1.0 - LAMBDA_INIT,
                                                                None, ALU.mult), [b_sublncol], [b_sublncol])
                    pN, b_pN = palloc(ph, "pN", [128, 512])
                    pSt = [palloc(ph, "pSt%d" % i, [128, 512]) for i in range(2)]
                    pAccs = [[palloc(ph, "pAcc%d%d" % (j, i), [128, 512]) for i in range(2)] for j in range(2)]
                    pJ = [pSt[0], pSt[1], palloc(ph, "pJ2", [128, 512]), (pN, b_pN), pAccs[0][0], pAccs[0][1]]
                    qg_i = [0]
                    qgroups = [(0, 4), (4, 4), (8, 4), (12, 4)]
                    pj_i = [0]

                    def next_pj():
                        r = pJ[pj_i[0] % 6]
                        pj_i[0] += 1
                        return r

                    def proj_stream(h):
                        QT, b_QT = QTs[0]
                        KT, b_KT = KTs[0]
                        Vaug, b_Vaug = Vaugs[0]
                        uQK, b_uQK = ring.load(U_QK[h])
                        cnt = 0
                        for tg in range(4):
                            tsl = slice(tg * 512, (tg + 1) * 512)
                            pas, qb_l = [], []
                            for which in range(2):
                                pa, b_pa = next_pj()
                                S.group("pe", [lambda kc=kc: nc.tensor.matmul(
                                    pa[:, :], uQK[:, kc * 512 + which * 128: kc * 512 + (which + 1) * 128],
                                    hT[:, kc, tsl], start=(kc == 0), stop=(kc == 7)) for kc in range(8)],
                                    b_hT[tg * 4:(tg + 1) * 4] + [b_uQK], [b_pa])
                                qb, b_qb = qbs[which]
                                S.op("act", lambda: nc.scalar.copy(qb[:], pa[:, :]), [b_pa], [b_qb])
                                pas.append((pa, b_pa))
                                qb_l.append((qb, b_qb))
                            pV, b_pV = next_pj()
                            for t in range(4):
                                S.group("pe", [lambda kc=kc: nc.tensor.matmul(
                                    pV[:, t * 128:(t + 1) * 128], hT[:, kc, (tg * 4 + t) * 128:(tg * 4 + t + 1) * 128],
                                    uQK[:, kc * 512 + 256: kc * 512 + 384],
                                    start=(kc == 0), stop=(kc == 7), skip_group_check=True) for kc in range(8)],
                                    [b_hT[tg * 4 + t], b_uQK], [b_pV])
                            S.op("act", lambda: nc.scalar.copy(
                                Vaug[:, tg * 4:(tg + 1) * 4, :], pV[:, :].rearrange("p (t e) -> p t e", t=4)),
                                [b_pV], [b_Vaug])
                            for which, (dst, b_dst) in enumerate(((QT, b_QT), (KT, b_KT))):
                                pa, b_pa = pas[which]
                                qb, b_qb = qb_l[which]
                                t1, b_t1 = t1s[which]
                                t2, b_t2 = t2s[which]
                                pr, b_pr = next_pj()
                                S.group("pe", [lambda: nc.tensor.matmul(pr[:, :], permb[:], qb[:], start=True, stop=True)],
                                        [b_permb, b_qb], [b_pr])
                                S.op("dve", lambda: nc.vector.tensor_tensor(
                                    t1[:], pa[:, :], rope[:, tg * 512:(tg + 1) * 512], ALU.mult),
                                    [b_pa, b_rope], [b_t1])
                                S.op("dve", lambda: nc.vector.tensor_tensor(
                                    t2[:], pr[:, :], rope[:, SEQ + tg * 512: SEQ + (tg + 1) * 512], ALU.mult),
                                    [b_pr, b_rope], [b_t2])
                                S.op("pool", lambda: nc.gpsimd.tensor_tensor(dst[:, tsl], t1[:], t2[:], ALU.add),
                                     [b_t1, b_t2], [b_dst])
                            yield

                    pending = []

                    def attn_stream(h):
                        QT, b_QT = QTs[0]
                        KT, b_KT = KTs[0]
                        Vaug, b_Vaug = Vaugs[0]
                        for (q0, nq) in qgroups:
                            nk = q0 + nq
                            W = nq * 128
                            pAcc = pAccs[qg_i[0] % 2]
                            qg_i[0] += 1

                            def emit_st(kt, i):
                                lo = max(0, kt - q0)
                                pX, b_pX = pSt[i]
                                fns = [lambda: nc.tensor.matmul(
                                    pX[:, lo * 128: W], KT[i * 64:(i + 1) * 64, kt * 128:(kt + 1) * 128],
                                    QT[i * 64:(i + 1) * 64, (q0 + lo) * 128:(q0 + nq) * 128],
                                    start=True, stop=(kt < q0))]
                                if kt >= q0:
                                    fns.append(lambda: nc.tensor.matmul(
                                        pX[:, lo * 128:(lo + 1) * 128], identb[:], negmb[:], start=False, stop=True))
                                S.group("pe", fns, [b_KT, b_QT, b_identb, b_negmb], [b_pX])

                            emit_st(0, 0)
                            emit_st(0, 1)
                            for kt in range(nk):
                                c0 = max(0, kt - q0) * 128
                                pts = []
                                for i in range(2):
                                    pX, b_pX = pSt[i]
                                    pTt_, b_pTt_ = pTs_[i][kt % 2]
                                    pts.append((pTt_, b_pTt_))
                                    S.op("act", lambda: nc.scalar.activation(
                                        out=pTt_[:, c0:W], in_=pX[:, c0:W], func=AF.Exp, scale=0.125), [b_pX], [b_pTt_])
                                    if kt + 1 < nk:
                                        emit_st(kt + 1, i)
                                    rsum, b_rsum = rsums[i]
                                    eng, e_ = ("dve", nc.vector) if i == 0 else ("pool", nc.gpsimd)
                                    if kt == 0:
                                        S.op(eng, lambda: e_.tensor_copy(rsum[:, 0:W], pTt_[:, 0:W]), [b_pTt_], [b_rsum])
                                    else:
                                        S.op(eng, lambda: e_.tensor_tensor(rsum[:, c0:W], rsum[:, c0:W], pTt_[:, c0:W],
                                                                           ALU.add), [b_pTt_, b_rsum], [b_rsum])
                                    yield
                                S.group("pe", [lambda i=i: nc.tensor.matmul(
                                    pAcc[i][0][:, c0:W], Vaug[:, kt, :], pts[i][0][:, c0:W],
                                    start=(kt == 0), stop=(kt == nk - 1), skip_group_check=True) for i in range(2)],
                                    [pts[0][1], pts[1][1], b_Vaug], [pAcc[0][1], pAcc[1][1]])
                                yield
                                if kt == 1 and pending:
                                    pending.pop(0)()
                            for i in range(2):
                                S.op("dve", lambda i=i: nc.vector.tensor_copy(rsbs[i][0][:, 0:W], rsums[i][0][:, 0:W]),
                                     [rsums[i][1]], [rsbs[i][1]])

                            def norm(h=h, q0=q0, nq=nq, W=W, pAcc=pAcc):
                                for i in range(2):
                                    rsb, b_rsb = rsbs[i]
                                    S.group("pe", [lambda: nc.tensor.matmul(pN[:, 0:W], onesb[:], rsb[:, 0:W],
                                                                            start=True, stop=True)],
                                            [b_onesb, b_rsb], [b_pN])
                                    S.op("act", lambda: nc.scalar.activation(out=rr[:, 0:W], in_=pN[:, 0:W], func=AF.Ln),
                                         [b_pN], [b_rr])
                                    S.op("act", lambda: nc.scalar.activation(out=rr[:, 0:W], in_=rr[:, 0:W], func=AF.Exp,
                                                                             scale=-1.0), [b_rr], [b_rr])
                                    oo, b_oo = (o1, b_o1) if i == 0 else (o2, b_o2)
                                    S.op("dve", lambda: nc.vector.tensor_tensor(oo[:, 0:W], pAcc[i][0][:, 0:W], rr[:, 0:W],
                                                                                ALU.mult), [pAcc[i][1], b_rr], [b_oo])
                                S.op("dve", lambda: nc.vector.scalar_tensor_tensor(
                                    o1[:, 0:W], o2[:, 0:W], neglam, o1[:, 0:W], ALU.mult, ALU.add),
                                    [b_o1, b_o2, b_small], [b_o1])
                                rsb, b_rsb = rsbs[0]
                                S.op("pool", lambda: nc.gpsimd.tensor_tensor(rsb[:, 0:W], o1[:, 0:W], o1[:, 0:W], ALU.mult),
                                     [b_o1], [b_rsb])
                                S.group("pe", [lambda: nc.tensor.matmul(pN[:, 0:W], onesb[:], rsb[:, 0:W],
                                                                        start=True, stop=True)], [b_onesb, b_rsb], [b_pN])
                                S.op("act", lambda: nc.scalar.activation(out=rr[:, 0:W], in_=pN[:, 0:W], func=AF.Ln,
                                                                         bias=nh384[:, 0:1], scale=1.0 / 128),
                                     [b_pN, b_nh384], [b_rr])
                                S.op("act", lambda: nc.scalar.activation(out=o2[:, 0:W], in_=rr[:, 0:W], func=AF.Exp,
                                                                         scale=-0.5), [b_rr], [b_o2])
                                S.op("dve", lambda: nc.vector.scalar_tensor_tensor(
                                    yattnT[:, h, q0 * 128: q0 * 128 + W], o1[:, 0:W], sublncol[:, 0:1], o2[:, 0:W],
                                    ALU.mult, ALU.mult), [b_o1, b_o2, b_sublncol], b_yattnT[q0:q0 + nq])

                            pending.append(norm)
                            yield

                    for _ in proj_stream(0):
                        pass
                    for h in range(8):
                        for _ in attn_stream(h):
                            pass
                        if h + 1 < 8:
                            for k_, _ in enumerate(proj_stream(h + 1)):
                                if k_ == 0 and pending:
                                    pending.pop(0)()
                        while pending:
                            pending.pop(0)()
                    S.barrier()
                if debug and s == 0:
                    out_evs.append(S.dma(dbg["d_yattnT"], yattnT[:].rearrange("p k t -> p (k t)"), reads=b_yattnT))

                if stop_after == "C":
                    raise _Stop()
                with contextlib.ExitStack() as ph:
                    ring = Ring(ph, 5, "d1")
                    tga, b_tga = alloc(ph, "tga", [128, 512])
                    tgb, b_tgb = alloc(ph, "tgb", [128, 512])
                    m1, b_m1 = alloc(ph, "m1", [128, 512])
                    m2, b_m2 = alloc(ph, "m2", [128, 512])
                    pP = [palloc(ph, "pM%d" % i, [128, 512]) for i in range(8)]
                    for j in range(2):
                        uGS, b_uGS = ring.load(U_GS[j])
                        uBRS, b_uBRS = ring.load(U_BRS[j])
                        uGA, b_uGA = ring.load(U_GA[j])
                        uBRA, b_uBRA = ring.load(U_BRA[j])
                        it = 0
                        for f4 in range(4):
                            fc = j * 4 + f4
                            for tg in range(4):
                                tsl = slice(tg * 512, (tg + 1) * 512)
                                base = (it % 2) * 4
                                it += 1
                                for k4, (unit, b_unit, src, b_src) in enumerate((
                                        (uGS, b_uGS, hT, b_hT), (uBRS, b_uBRS, yssdT, b_yssdT),
                                        (uGA, b_uGA, hT, b_hT), (uBRA, b_uBRA, yattnT, b_yattnT))):
                                    pX, b_pX = pP[base + k4]
                                    S.group("pe", [lambda kc=kc, pX=pX, unit=unit, src=src: nc.tensor.matmul(
                                        pX[:, :], unit[:, kc * 512 + f4 * 128: kc * 512 + (f4 + 1) * 128],
                                        src[:, kc, tsl], start=(kc == 0), stop=(kc == 7)) for kc in range(8)],
                                        b_src[tg * 4:(tg + 1) * 4] + [b_unit], [b_pX])
                                S.op("act", lambda base=base: nc.scalar.activation(
                                    out=tga[:], in_=pP[base][0][:, :], func=AF.Tanh, scale=0.5), [pP[base][1]], [b_tga])
                                S.op("dve", lambda base=base: nc.vector.scalar_tensor_tensor(
                                    m1[:], tga[:], 1.0, pP[base + 1][0][:, :], ALU.add, ALU.mult),
                                    [b_tga, pP[base + 1][1]], [b_m1])
                                S.op("act", lambda base=base: nc.scalar.activation(
                                    out=tgb[:], in_=pP[base + 2][0][:, :], func=AF.Tanh, scale=0.5),
                                    [pP[base + 2][1]], [b_tgb])
                                S.op("dve", lambda base=base: nc.vector.scalar_tensor_tensor(
                                    m2[:], tgb[:], 1.0, pP[base + 3][0][:, :], ALU.add, ALU.mult),
                                    [b_tgb, pP[base + 3][1]], [b_m2])
                                S.op("pool", lambda fc=fc: nc.gpsimd.tensor_tensor(mergedT[:, fc, tsl], m1[:], m2[:], ALU.add),
                                     [b_m1, b_m2], [b_mergedT])
                    S.barrier()
                if debug and s == 0:
                    out_evs.append(S.dma(dbg["d_mergedT"], mergedT[:].rearrange("p k t -> p (k t)"), reads=[b_mergedT]))

            if stop_after == "D1":
                raise _Stop()
            sq.close()

            with contextlib.ExitStack() as ph:
                ring = Ring(ph, 5, "d2")
                rowd, b_rowp = alloc(ph, "rowd", [128, 2048])
                S.dma(rowd[:], rowp_d[:, 1024:3072].partition_broadcast(128), writes=[b_rowp])
                wffn_bc = rowd[:, 0:1024]
                wfin_bc = rowd[:, 1024:2048]
                dgf, b_dgf = alloc(ph, "dgfb", [128, NFC * 384], BF16)
                S.dma(dgf[:], dsc_f, reads=[B_dsc_f], writes=[b_dgf])
                halo, b_halo = alloc(ph, "halo", [128, NFC, 2], BF16)
                S.op("pool", lambda: nc.gpsimd.memset(halo[:], 0.0), [], [b_halo])
                x1, _ = alloc(ph, "x1", [128, 4, DM])
                b_x1 = [Buf("x1_%d" % t) for t in range(4)]
                h2T, _ = alloc(ph, "h2T", [128, 8, 512], BF16)
                b_h2T = [Buf("h2T%d" % t) for t in range(4)]
                actT, _ = alloc(ph, "actT", [128, NFC, 512], BF16)
                b_actT = [Buf("actT%d" % c) for c in range(NFC)]
                gins = [alloc(ph, "gin%d" % i, [128, 514], BF16) for i in range(3)]
                xts = [alloc(ph, "xd%d" % i, [128, DM]) for i in range(4)]
                hbs = [alloc(ph, "hbd%d" % i, [128, DM], BF16) for i in range(2)]
                st2s = [alloc(ph, "std%d" % i, [128, 4]) for i in range(2)]
                thf, b_thf = alloc(ph, "thf", [128, 512])
                sg, b_sg = alloc(ph, "sg", [128, 512])
                ots = [alloc(ph, "ot%d" % i, [128, DM]) for i in range(2)]
                pQ = [palloc(ph, "pQ%d" % i, [128, 512]) for i in range(7)]
                psT, b_psT = palloc(ph, "psTd", [128, 1024], BF16)
                for tg in range(4):
                    uO = [ring.load(U_OUT[0]), ring.load(U_OUT[1])]
                    for t in range(4):
                        tok = tg * 4 + t
                        xt, b_xt = xts[t % 4]
                        S.dma(xt[:], x_d[s, tok * 128:(tok + 1) * 128, :], writes=[b_xt], q="pool")
                        for half in range(2):
                            pX, b_pX = pQ[half]
                            S.group("pe", [lambda kc=kc, pX=pX, half=half: nc.tensor.matmul(
                                pX[:, :], mergedT[:, kc, tok * 128:(tok + 1) * 128],
                                uO[half][0][:, kc * 512:(kc + 1) * 512], start=(kc == 0), stop=(kc == 7))
                                for kc in range(8)], [b_mergedT, uO[half][1]], [b_pX])
                            S.op("dve", lambda pX=pX, half=half, xt=xt: nc.vector.scalar_tensor_tensor(
                                x1[:, t, half * 512:(half + 1) * 512], pX[:, :], 0.5, xt[:, half * 512:(half + 1) * 512],
                                ALU.mult, ALU.add), [b_pX, b_xt], [b_x1[t]])
                        rms_to_T(x1[:, t, :], b_x1[t], wffn_bc, b_rowp, h2T, b_h2T[t], t * 128, hbs[t % 2][0],
                                 hbs[t % 2][1], st2s[t % 2][0], st2s[t % 2][1], psT, b_psT,
                                 "act" if t % 2 == 0 else "dve")
                    d3_pending = []
                    for j in range(11):
                        uU, b_uU = ring.load(U_UP[j])
                        for sub in range(2):
                            ch = 2 * j + sub
                            pG, b_pG = pQ[ch % 2]
                            pVv, b_pVv = pQ[2 + ch % 3]
                            pCv, b_pCv = pQ[5 + ch % 2]
                            gin, b_gin = gins[ch % 3]
                            S.group("pe", [lambda kc=kc, pG=pG, sub=sub: nc.tensor.matmul(
                                pG[:, :], uU[:, kc * 512 + sub * 128: kc * 512 + (sub + 1) * 128], h2T[:, kc, :],
                                start=(kc == 0), stop=(kc == 7)) for kc in range(8)], b_h2T + [b_uU], [b_pG])
                            S.group("pe", [lambda kc=kc, pVv=pVv, sub=sub: nc.tensor.matmul(
                                pVv[:, :], uU[:, kc * 512 + 256 + sub * 128: kc * 512 + 256 + (sub + 1) * 128],
                                h2T[:, kc, :], start=(kc == 0), stop=(kc == 7)) for kc in range(8)],
                                b_h2T + [b_uU], [b_pVv])
                            S.op("pool", lambda gin=gin, ch=ch: nc.gpsimd.tensor_copy(gin[:, 0:2], halo[:, ch, :]),
                                 [b_halo], [b_gin])
                            S.op("act", lambda gin=gin, pG=pG: nc.scalar.copy(gin[:, 2:514], pG[:, :]), [b_pG], [b_gin])
                            S.op("pool", lambda gin=gin, ch=ch: nc.gpsimd.tensor_copy(halo[:, ch, :], gin[:, 512:514]),
                                 [b_gin], [b_halo])
                            def conv_tail(ch=ch, pCv=pCv, b_pCv=b_pCv, gin=gin, b_gin=b_gin, pVv=pVv, b_pVv=b_pVv):
                                fns = [lambda j3=j3: nc.tensor.matmul(
                                    pCv[:, :], dgf[:, ch * 384 + j3 * 128: ch * 384 + (j3 + 1) * 128],
                                    gin[:, j3: j3 + 512], start=(j3 == 0), stop=False) for j3 in range(3)]
                                fns.append(lambda: nc.tensor.matmul(
                                    pCv[:, :], cbf_b[0:1, ch * 128:(ch + 1) * 128], onesrow[0:1, :], start=False, stop=True))
                                S.group("pe", fns, [b_gin, b_dgf, b_cbf, b_onesrow], [b_pCv])
                                S.op("act", lambda: nc.scalar.activation(out=thf[:], in_=pCv[:, :], func=AF.Tanh),
                                     [b_pCv], [b_thf])
                                S.op("dve", lambda: nc.vector.scalar_tensor_tensor(
                                    sg[:], thf[:], 1.0, pCv[:, :], ALU.add, ALU.mult), [b_thf, b_pCv], [b_sg])
                                S.op("dve", lambda: nc.vector.tensor_tensor(
                                    actT[:, ch, :], sg[:], pVv[:, :], ALU.mult), [b_sg, b_pVv], [b_actT[ch]])

                            if d3_pending:
                                d3_pending.pop(0)()
                            d3_pending.append(conv_tail)
                    while d3_pending:
                        d3_pending.pop(0)()
                    for half in range(2):
                        for ku in range(3):
                            nk = 8 if ku < 2 else 6
                            uD, b_uD = ring.load(U_DN[half * 3 + ku], n=nk * 512)
                            for t in range(4):
                                pX, b_pX = pQ[t]
                                S.group("pe", [lambda kcl=kcl, pX=pX, t=t, ku=ku, nk=nk, uD=uD: nc.tensor.matmul(
                                    pX[:, :], actT[:, ku * 8 + kcl, t * 128:(t + 1) * 128],
                                    uD[:, kcl * 512:(kcl + 1) * 512], start=(ku == 0 and kcl == 0),
                                    stop=(ku == 2 and kcl == nk - 1), skip_group_check=True) for kcl in range(nk)],
                                    b_actT[ku * 8: ku * 8 + nk] + [b_uD], [b_pX])
                        for t in range(4):
                            pX, b_pX = pQ[t]
                            S.op("dve", lambda pX=pX, t=t, half=half: nc.vector.tensor_tensor(
                                x1[:, t, half * 512:(half + 1) * 512], pX[:, :], x1[:, t, half * 512:(half + 1) * 512],
                                ALU.add), [b_pX, b_x1[t]], [b_x1[t]])
                    for t in range(4):
                        tok = tg * 4 + t
                        st2, b_st2 = st2s[t % 2]
                        ot, b_ot = ots[t % 2]
                        S.op("act", lambda t=t, st2=st2: nc.scalar.activation(
                            out=junk[:], in_=x1[:, t, :], func=AF.Square, accum_out=st2[:, 0:1]),
                            [b_x1[t]], [b_junk, b_st2])
                        S.op("dve", lambda st2=st2: nc.vector.tensor_scalar(st2[:, 1:2], st2[:, 0:1], 1.0 / DM, NORM_EPS,
                                                                            ALU.mult, ALU.add), [b_st2], [b_st2])
                        S.op("pool", lambda st2=st2: nc.gpsimd.tensor_tensor(st2[:, 2:3], st2[:, 1:2], neghalf[:, 0:1],
                                                                             ALU.pow), [b_st2, b_small], [b_st2])
                        S.op("dve", lambda t=t, st2=st2, ot=ot: nc.vector.scalar_tensor_tensor(
                            ot[:], x1[:, t, :], st2[:, 2:3], wfin_bc, ALU.mult, ALU.mult),
                            [b_x1[t], b_st2, b_rowp], [b_ot])
                        out_evs.append(S.dma(out_d[s, tok * 128:(tok + 1) * 128, :], ot[:], reads=[b_ot], q="pool"))
                S.barrier()
            so.close()
          except _Stop:
            sq.close()
            so.close()
            break

        for ev in out_evs:
            S._wait("sp", ev)
    return nc, S


def _host_layout(inp):
    f = np.float32
    W = np.asarray(inp["w_in"], f)[0]
    cols = []
    for g in range(2):
        cols.append(W[:, 1024 + g * 512: 1024 + (g + 1) * 512])
    cols.append(np.concatenate([W[:, 2048:2176], W[:, 2304:2432], W[:, 2176:2304], W[:, 2432:2560]], axis=1))
    for g in range(2):
        cols.append(W[:, g * 512:(g + 1) * 512])
    for h in range(8):
        q = W[:, 2576 + h * 128: 2576 + (h + 1) * 128]
        k = W[:, 3600 + h * 128: 3600 + (h + 1) * 128]
        v = W[:, 4624 + h * 128: 4624 + (h + 1) * 128]
        cols.append(np.concatenate([q, k, v, np.zeros_like(v)], axis=1))
    for j in range(2):
        cols.append(W[:, 4624 + j * 512: 4624 + (j + 1) * 512])
    for j in range(2):
        cols.append(W[:, 5648 + j * 512: 5648 + (j + 1) * 512])
    for j in range(2):
        cols.append(W[:, 6672 + j * 512: 6672 + (j + 1) * 512])
    for nm in ("w_branch_ssd", "w_branch_attn", "w_out"):
        M = np.asarray(inp[nm], f)[0]
        for j in range(2):
            cols.append(M[:, j * 512:(j + 1) * 512])
    U = np.asarray(inp["w_up"], f)[0]
    for j in range(11):
        cols.append(np.concatenate([U[:, (2 * j) * 128:(2 * j + 1) * 128], U[:, (2 * j + 1) * 128:(2 * j + 2) * 128],
                                    U[:, DFF + (2 * j) * 128: DFF + (2 * j + 1) * 128],
                                    U[:, DFF + (2 * j + 1) * 128: DFF + (2 * j + 2) * 128]], axis=1))
    wcat = np.ascontiguousarray(np.concatenate(cols, axis=1))
    assert wcat.shape == (DM, NU1 * 512)
    wdt = np.ascontiguousarray(W[:, 2560:2576].reshape(8, 128, 16).transpose(1, 0, 2).reshape(128, 128))
    rowp = np.concatenate([np.asarray(inp["norm_mix_w"], f)[0], np.asarray(inp["norm_ffn_w"], f)[0],
                           np.asarray(inp["final_norm_w"], f), np.asarray(inp["ssd_norm_w"], f)[0],
                           np.repeat(np.asarray(inp["ssd_d_skip"], f)[0], 64)])[None, :]
    rows = np.concatenate([np.asarray(inp["ssd_dt_bias"], f)[0], np.asarray(inp["ssd_a_log"], f)[0],
                           np.asarray(inp["subln_w"], f)[0], np.asarray(inp["lambda_q1"], f)[0],
                           np.asarray(inp["lambda_k1"], f)[0], np.asarray(inp["lambda_q2"], f)[0],
                           np.asarray(inp["lambda_k2"], f)[0]])[None, :]
    chbase = [i * 128 for i in range(8)] + [1024, 1152, 1280, 1408]
    cw = np.asarray(inp["ssd_conv_w"], f)[0]
    cb = np.asarray(inp["ssd_conv_b"], f)[0]
    convs = np.zeros((128, 48), f)
    cbs = np.zeros((1, 1536), f)
    for ch, b0 in enumerate(chbase):
        convs[:, ch * 4:(ch + 1) * 4] = cw[:, b0:b0 + 128].T
        cbs[0, ch * 128:(ch + 1) * 128] = cb[b0:b0 + 128]
    fw = np.asarray(inp["ffn_conv_w"], f)[0]
    convf = np.zeros((128, NFC * 3), f)
    for ch in range(NFC):
        convf[:, ch * 3:(ch + 1) * 3] = fw[:, ch * 128:(ch + 1) * 128].T
    cbf = np.asarray(inp["ffn_conv_b"], f)[0][None, :]
    idx = np.arange(128)
    ident = np.eye(128, dtype=f)
    negm = np.where(idx[:, None] <= idx[None, :], 0.0, -30000.0).astype(f)
    tri = (idx[:, None] <= idx[None, :]).astype(f)
    gt = (idx[:, None] > idx[None, :]).astype(f)
    src = np.array([(c // 64) * 64 + ((c % 64) + 32) % 64 for c in range(128)])
    permm = np.zeros((128, 128), f)
    permm[src, idx] = 1.0
    cmat = np.concatenate([ident, negm, tri, gt, np.ones((128, 128), f), permm], axis=1)
    onehot = np.zeros((16, 16 * 128), f)
    for hh in range(16):
        onehot[hh, hh * 128:(hh + 1) * 128] = 1.0
    inv = (1.0 / (np.float32(10000.0) ** (np.arange(0, 64, 2, dtype=f) / np.float32(64)))).astype(f)
    ang = (np.arange(SEQ, dtype=f)[:, None] * inv[None, :]).astype(f)
    cos, sin = np.cos(ang).astype(f), np.sin(ang).astype(f)
    rope = np.zeros((128, 2 * SEQ), f)
    for p in range(128):
        dd = p % 64
        rope[p, 0:SEQ] = cos[:, dd % 32]
        rope[p, SEQ:] = (-1.0 if dd < 32 else 1.0) * sin[:, dd % 32]
    return dict(wcat=wcat, wdown=np.ascontiguousarray(np.asarray(inp["w_down"], f)[0]), wdt=wdt,
                rowp=np.ascontiguousarray(rowp), rows=np.ascontiguousarray(rows), convs=convs, convf=convf,
                cbs=cbs, cbf=np.ascontiguousarray(cbf), cmat=np.ascontiguousarray(cmat), onehot=onehot, rope=rope)


def kernel(**inputs):
    x = np.asarray(inputs["x"], np.float32)
    nb = x.shape[0] // NCORES
    shared = _host_layout(inputs)
    nc = build_program(nseq=nb)
    in_maps = []
    for c in range(NCORES):
        m = dict(shared)
        m["x"] = np.ascontiguousarray(x[c * nb:(c + 1) * nb])
        in_maps.append(m)
    res = run_bass_kernel_spmd(nc, in_maps, core_ids=list(range(NCORES)))
    return np.concatenate([np.asarray(r["out"], np.float32) for r in res.results], axis=0)
```
